# Optimizing a Trainium2 kernel written in Bass

```python
import jax, jax.numpy as jnp
from jax import lax
import numpy as np

D_MODEL = 1024
BATCH = 16
SEQ = 4096
DEPTH = 1
DEC_BATCH = 8
DEC_SEQ = 32
PAST_LEN = 2048

CHUNK = 64
SB_HEADS = 8
SB_HEAD_DIM = 64
SB_DIM = SB_HEADS * SB_HEAD_DIM
CONV_DIM = D_MODEL - SB_DIM
CONV_WIDTH = 31
FFN_CONV_WIDTH = 3
D_FF = 2816
Q_BLOCK = 128
IN_DIM = 3 * SB_DIM + 2 * CONV_DIM
EPS = 1e-6

kernel_name = "hybrid_stickbreaking_conformer_conv_step"


def rmsnorm(x, g):
    xf = x.astype(jnp.float32)
    xf = xf * lax.rsqrt(jnp.mean(xf * xf, axis=-1, keepdims=True) + EPS)
    return (xf * g.astype(jnp.float32)).astype(x.dtype)


def layernorm(x, g, b):
    xf = x.astype(jnp.float32)
    mu = jnp.mean(xf, axis=-1, keepdims=True)
    xc = xf - mu
    xf = xc * lax.rsqrt(jnp.mean(xc * xc, axis=-1, keepdims=True) + EPS)
    return (xf * g.astype(jnp.float32) + b.astype(jnp.float32)).astype(x.dtype)


def causal_dwconv(x, state, w, b):
    width = w.shape[0]
    xp = jnp.concatenate([state.astype(x.dtype), x], axis=1)
    out = lax.conv_general_dilated(
        xp, w[:, None, :].astype(x.dtype), window_strides=(1,), padding='VALID',
        dimension_numbers=('NWC', 'WIO', 'NWC'), feature_group_count=x.shape[-1])
    return out + b.astype(x.dtype), xp[:, xp.shape[1] - (width - 1):]


def sb_attend(q, k, v, q_pos, k_pos):
    z = jnp.einsum('bqhd,bkhd->bhqk', q, k).astype(jnp.float32) * (SB_HEAD_DIM ** -0.5)
    mask = k_pos[None, :] < q_pos[:, None]
    sp = jnp.where(mask, jax.nn.softplus(z), 0.0)
    surv_incl = lax.cumsum(sp, axis=3, reverse=True)
    log_a = jax.nn.log_sigmoid(z) - (surv_incl - sp)
    a = jnp.where(mask, jnp.exp(log_a), 0.0)
    return jnp.einsum('bhqk,bkhd->bqhd', a.astype(v.dtype), v)


def head_rmsnorm(x, g):
    xf = x.astype(jnp.float32)
    xf = xf * lax.rsqrt(jnp.mean(xf * xf, axis=-1, keepdims=True) + EPS)
    return (xf * g.astype(jnp.float32)).astype(x.dtype)


def layer(x, past_k, past_v, conv_state, ffn_state, g_mix, w_in, g_q, g_k, w_dw, b_dw,
          g_conv_ln, b_conv_ln, w_out, g_ffn, w_up, w_ffn_dw, b_ffn_dw, w_down):
    B, T, _ = x.shape
    P = past_k.shape[1]
    xn = rmsnorm(x, g_mix)
    proj = xn @ w_in.astype(x.dtype)
    q, k, v, u, gate = jnp.split(
        proj, [SB_DIM, 2 * SB_DIM, 3 * SB_DIM, 3 * SB_DIM + CONV_DIM], axis=-1)
    q = head_rmsnorm(q.reshape(B, T, SB_HEADS, SB_HEAD_DIM), g_q)
    k = head_rmsnorm(k.reshape(B, T, SB_HEADS, SB_HEAD_DIM), g_k)
    v = v.reshape(B, T, SB_HEADS, SB_HEAD_DIM)
    k_all = jnp.concatenate([past_k.astype(k.dtype), k], axis=1)
    v_all = jnp.concatenate([past_v.astype(v.dtype), v], axis=1)
    k_pos = jnp.arange(P + T)
    q_pos = P + jnp.arange(T)
    if T > Q_BLOCK and T % Q_BLOCK == 0:
        nb = T // Q_BLOCK
        qb = q.reshape(B, nb, Q_BLOCK, SB_HEADS, SB_HEAD_DIM).swapaxes(0, 1)
        pb = q_pos.reshape(nb, Q_BLOCK)
        ob = lax.map(lambda a: sb_attend(a[0], k_all, v_all, a[1], k_pos), (qb, pb))
        o = ob.swapaxes(0, 1).reshape(B, T, SB_DIM)
    else:
        o = sb_attend(q, k_all, v_all, q_pos, k_pos).reshape(B, T, SB_DIM)
    c = u * jax.nn.sigmoid(gate)
    c, new_conv_state = causal_dwconv(c, conv_state, w_dw, b_dw)
    c = jax.nn.silu(layernorm(c, g_conv_ln, b_conv_ln))
    h = x + jnp.concatenate([o, c], axis=-1) @ w_out.astype(x.dtype)
    up = rmsnorm(h, g_ffn) @ w_up.astype(x.dtype)
    up_c, new_ffn_state = causal_dwconv(up, ffn_state, w_ffn_dw, b_ffn_dw)
    a, g = jnp.split(up_c, 2, axis=-1)
    y = h + (jax.nn.silu(g) * a) @ w_down.astype(x.dtype)
    return y, k, v, new_conv_state, new_ffn_state


def setup_inputs(seed: int = 0) -> dict:
    key = jax.random.key(seed)
    ks = jax.random.split(key, 20)
    f32 = jnp.float32
    n = lambda i, shape, s: jax.random.normal(ks[i], shape, f32) * s
    return {
        "x_prompt": n(0, (BATCH, SEQ, D_MODEL), 1.0),
        "x_sample": n(1, (DEC_BATCH, DEC_SEQ, D_MODEL), 1.0),
        "cache_sb_k": n(2, (DEPTH, DEC_BATCH, PAST_LEN, SB_HEADS, SB_HEAD_DIM), 1.0),
        "cache_sb_v": n(3, (DEPTH, DEC_BATCH, PAST_LEN, SB_HEADS, SB_HEAD_DIM), 1.0),
        "state_conv": n(4, (DEPTH, DEC_BATCH, CONV_WIDTH - 1, CONV_DIM), 1.0),
        "state_ffn_conv": n(5, (DEPTH, DEC_BATCH, FFN_CONV_WIDTH - 1, 2 * D_FF), 1.0),
        "g_mix": 1.0 + n(6, (DEPTH, D_MODEL), 0.05),
        "w_in": n(7, (DEPTH, D_MODEL, IN_DIM), D_MODEL ** -0.5),
        "g_q": 1.0 + n(8, (DEPTH, SB_HEAD_DIM), 0.05),
        "g_k": 1.0 + n(9, (DEPTH, SB_HEAD_DIM), 0.05),
        "w_dw": n(10, (DEPTH, CONV_WIDTH, CONV_DIM), CONV_WIDTH ** -0.5),
        "b_dw": n(11, (DEPTH, CONV_DIM), 0.02),
        "g_conv_ln": 1.0 + n(12, (DEPTH, CONV_DIM), 0.05),
        "b_conv_ln": n(13, (DEPTH, CONV_DIM), 0.02),
        "w_out": n(14, (DEPTH, D_MODEL, D_MODEL), D_MODEL ** -0.5),
        "g_ffn": 1.0 + n(15, (DEPTH, D_MODEL), 0.05),
        "w_up": n(16, (DEPTH, D_MODEL, 2 * D_FF), D_MODEL ** -0.5),
        "w_ffn_dw": n(17, (DEPTH, FFN_CONV_WIDTH, 2 * D_FF), FFN_CONV_WIDTH ** -0.5),
        "b_ffn_dw": n(18, (DEPTH, 2 * D_FF), 0.02),
        "w_down": n(19, (DEPTH, D_FF, D_MODEL), D_FF ** -0.5),
    }


def reference(x_prompt, x_sample, cache_sb_k, cache_sb_v, state_conv, state_ffn_conv,
              g_mix, w_in, g_q, g_k, w_dw, b_dw, g_conv_ln, b_conv_ln, w_out,
              g_ffn, w_up, w_ffn_dw, b_ffn_dw, w_down):
    xp, xs = x_prompt, x_sample
    Bp = xp.shape[0]
    kp_l, vp_l, ks_l, vs_l, cp_l, cs_l, fp_l, fs_l = [], [], [], [], [], [], [], []
    for l in range(DEPTH):
        w = (g_mix[l], w_in[l], g_q[l], g_k[l], w_dw[l], b_dw[l], g_conv_ln[l], b_conv_ln[l],
             w_out[l], g_ffn[l], w_up[l], w_ffn_dw[l], b_ffn_dw[l], w_down[l])
        empty = jnp.zeros((Bp, 0, SB_HEADS, SB_HEAD_DIM), xp.dtype)
        zc = jnp.zeros((Bp, CONV_WIDTH - 1, CONV_DIM), xp.dtype)
        zf = jnp.zeros((Bp, FFN_CONV_WIDTH - 1, 2 * D_FF), xp.dtype)
        xp, kp, vp, cp, fp = layer(xp, empty, empty, zc, zf, *w)
        xs, ks_, vs_, cs, fs = layer(xs, cache_sb_k[l], cache_sb_v[l], state_conv[l],
                                     state_ffn_conv[l], *w)
        kp_l.append(kp); vp_l.append(vp); ks_l.append(ks_); vs_l.append(vs_)
        cp_l.append(cp); cs_l.append(cs); fp_l.append(fp); fs_l.append(fs)
    return (xp, xs, jnp.stack(kp_l), jnp.stack(vp_l), jnp.stack(ks_l), jnp.stack(vs_l),
            jnp.stack(cp_l), jnp.stack(cs_l), jnp.stack(fp_l), jnp.stack(fs_l))
```

```python
import contextlib
import numpy as np
import concourse.bass as bass
import concourse.mybir as mybir
from concourse.bass_utils import run_bass_kernel_spmd

F32 = mybir.dt.float32
BF16 = mybir.dt.bfloat16
F32R = mybir.dt.float32r
AF = mybir.ActivationFunctionType
ALU = mybir.AluOpType
AX = mybir.AxisListType

D = 1024
IN = 2560
DFF = 2816
NFC = 44
EPS = 1e-6
SB_BASE = 16512
SB_END = 229344
ENGS = ['pe', 'act', 'dve', 'pool', 'sp']


class Prog:
    def __init__(self):
        self.ins = []
        self.lw = {}
        self.rd = {}
        self.last_eng = {}
        self.pending_dma = []
        self.bar = None
        self.dma_keys = {}

    def add(self, eng, fn, r=(), w=(), dma=None):
        al = getattr(self, 'alias', None)
        if al:
            r = list(r) + [p for k in r for p in al.get(k, ())]
            w = list(w) + [p for k in w for p in al.get(k, ())]
        i = len(self.ins)
        deps = set()
        if self.bar is not None:
            deps.add(self.bar)
        for k in r:
            x = self.lw.get(k)
            if x is not None:
                deps.add(x)
        for k in w:
            x = self.lw.get(k)
            if x is not None:
                deps.add(x)
            rr = self.rd.get(k)
            if rr:
                deps.update(rr.values())
        rk = ('dma', i) if dma is not None else eng
        for k in r:
            self.rd.setdefault(k, {})[rk] = i
        for k in w:
            self.lw[k] = i
            self.rd[k] = {}
        if dma is not None:
            if dma not in self.dma_keys:
                self.dma_keys[dma] = len(self.dma_keys)
            self.pending_dma.append(i)
        self.last_eng[eng] = i
        self.ins.append(dict(eng=eng, fn=fn, deps=deps, dma=dma))
        return i

    def barrier(self):
        deps = set(self.last_eng.values()) | set(self.pending_dma)
        i = len(self.ins)
        self.ins.append(dict(eng='sp', fn=lambda e: e.nop(), deps=deps, dma=None))
        self.last_eng = {'sp': i}
        self.pending_dma = []
        self.bar = i
        self.lw = {}
        self.rd = {}

    def emit(self, nc, stack):
        n = len(self.ins)
        sig = [False] * n
        for ins in self.ins:
            for d in ins['deps']:
                di = self.ins[d]
                if di['dma'] is not None:
                    continue
                if di['eng'] == 'pe' and ins['eng'] == 'pe':
                    continue
                sig[d] = True
        esem = {e: stack.enter_context(nc.semaphore("s_" + e)) for e in ENGS}
        dsem = {k: stack.enter_context(nc.semaphore("d_%d" % j)) for k, j in self.dma_keys.items()}
        cnt = {e: 0 for e in ENGS}
        dcnt = {k: 0 for k in self.dma_keys}
        sv = [None] * n
        for i, ins in enumerate(self.ins):
            if ins['dma'] is not None:
                dcnt[ins['dma']] += 16
                sv[i] = (dsem[ins['dma']], dcnt[ins['dma']], ('d', ins['dma']))
            elif sig[i]:
                cnt[ins['eng']] += 1
                sv[i] = (esem[ins['eng']], cnt[ins['eng']], ('e', ins['eng']))
        per = {e: [] for e in ENGS}
        for i, ins in enumerate(self.ins):
            per[ins['eng']].append(i)
        insl = self.ins

        def run(engname, eng):
            seen = {}
            for i in per[engname]:
                ins = insl[i]
                need = {}
                for d in ins['deps']:
                    s = sv[d]
                    if s is None:
                        continue
                    if need.get(s[2], (None, 0))[1] < s[1]:
                        need[s[2]] = (s[0], s[1])
                for key, (sem, val) in need.items():
                    if seen.get(key, 0) < val:
                        eng.wait_ge(sem, val)
                        seen[key] = val
                bi = ins['fn'](eng)
                if ins['dma'] is not None:
                    bi.then_inc(dsem[ins['dma']], 16)
                elif sig[i]:
                    bi.then_inc(esem[engname], 1)

        with nc.Block() as block:
            @block.tensor
            def _(e):
                run('pe', e)

            @block.scalar
            def _(e):
                run('act', e)

            @block.vector
            def _(e):
                run('dve', e)

            @block.gpsimd
            def _(e):
                run('pool', e)

            @block.sync
            def _(e):
                run('sp', e)


def build_nc(NSEQ, T, P):
    nc = bass.Bass("TRN2", target_bir_lowering=False)
    NPB = P // 128
    NTOK = NSEQ * T
    TS = 32

    def din(name, shape):
        return nc.dram_tensor(name, list(shape), F32, kind="ExternalInput").ap()

    def dout(name, shape):
        return nc.dram_tensor(name, list(shape), F32, kind="ExternalOutput").ap()

    xp = din("xp", [NTOK, D])
    xs = din("xs", [TS, D])
    ckT = din("ckT", [128, 4, P])
    cv = din("cv", [P, 512])
    sconvT = din("sconvT", [128, 4, 30])
    sffnT = din("sffnT", [128, NFC, 2])
    w_in = din("w_in", [D, IN])
    w_out = din("w_out", [D, D])
    w_up = din("w_up", [D, 2 * DFF])
    w_down = din("w_down", [DFF, D])
    consts = din("consts", [128, 4, 128])
    sm = din("smallp", [128, 512])
    yp = dout("yp", [NTOK, D])
    ys = dout("ys", [TS, D])
    kp = dout("kp", [NTOK, 512])
    vp = dout("vp", [NTOK, 512])
    ks = dout("ks", [TS, 512])
    vs = dout("vs", [TS, 512])
    cstp = dout("cstp", [NSEQ, 128, 4, 30])
    csts = dout("csts", [128, 4, 30])
    fstp = dout("fstp", [NSEQ, 128, NFC, 2])
    fsts = dout("fsts", [128, NFC, 2])
    hscr = nc.dram_tensor("hscr", [NTOK + TS, D], F32, kind="Internal").ap()

    off = [SB_BASE]
    sb_map = {}

    def sb(name, shape, dt, at=None):
        if at is None:
            at = off[0]
        nbytes = int(np.prod(shape[1:])) * (4 if dt == F32 else 2)
        nbytes = (nbytes + 31) // 32 * 32
        t = nc.alloc_sbuf_tensor_at(name, list(shape), dt, offset=at)
        assert at + nbytes <= SB_END, (name, at, nbytes)
        sb_map[name] = (at, nbytes)
        if at == off[0]:
            off[0] += nbytes
        return t

    c_f32 = sb("c_f32", [128, 4, 128], F32)
    smp = sb("smp", [128, 512], F32)
    Lb = sb("Lb", [128, 128], BF16)
    onesb = sb("onesb", [128, 128], BF16)
    zerosb = sb("zerosb", [128, 64], BF16)
    identb = sb("identb", [128, 128], BF16)
    negmb = sb("negmb", [128, 128], BF16)
    onesnb = sb("onesnb", [128, 128], BF16)
    mask2 = sb("mask2", [128, 2, 128], F32)
    stat = sb("stat", [128, 64], F32)
    halo = sb("halo", [128, NFC, 2], F32)
    ident = c_f32[:, 0, :]
    onesf = c_f32[:, 2, :]
    gmix = smp[:, 0:8]
    gffn = smp[:, 8:16]
    gq = smp[:, 16:80]
    gk = smp[:, 80:144]
    wdw = smp[:, 144:268].rearrange("p (c t) -> p c t", t=31)
    bdw = smp[:, 268:272]
    gln = smp[:, 272:276]
    bln = smp[:, 276:280]
    wfd = smp[:, 280:412].rearrange("p (c t) -> p c t", t=3)
    bfd = smp[:, 412:456]
    PH = off[0]

    off[0] = PH
    w_in_sb = sb("w_in_sb", [128, 8, IN], BF16)
    w_out_sb = sb("w_out_sb", [128, 8, D], BF16)
    KTC = max(T, P + 128)
    NVB = max(T // 128, NPB + 1)
    kT = sb("kT", [128, 4, KTC], BF16)
    v_sb = sb("v_sb", [128, NVB, 512], BF16)
    qT = sb("qT", [128, 4, 512], BF16)
    ccT = sb("ccT", [128, 4, 512], BF16)
    attnT = sb("attnT", [128, 4, 512], BF16)
    cbuf = sb("cbuf", [128, 4, 542], F32)
    xnT = sb("xnT", [128, 8, 512], BF16)
    cvb = sb("cvb", [128, 4, 512], F32)
    OV = off[0]
    sq2 = [sb("sq%d" % i, [128, 512], F32) for i in range(2)]
    tq2 = [sb("tq%d" % i, [128, 512], F32) for i in range(2)]
    kst = [sb("kst%d" % i, [128, 512], F32) for i in range(2)]
    vst = [sb("vst%d" % i, [128, 512], F32) for i in range(2)]
    sg = [sb("sg%d" % i, [128, 512], F32) for i in range(2)]
    lnm = sb("lnm", [128, 512], F32)
    lnr = sb("lnr", [128, 512], F32)
    lnt_off = off[0]
    lnt = [sb("lnt%d" % i, [128, 512], F32) for i in range(2)]
    assert off[0] <= OV + 30720
    off[0] = OV + 30720
    xring = [sb("xring0", [128, D], F32)]
    xnb4A = sb("xnb4A", [128, 4, D], BF16)
    endA1 = off[0]
    off[0] = OV
    e_b = [sb("e%d" % i, [128, 2, 512], F32) for i in range(3)]
    sp_b = [sb("sp%d" % i, [128, 2, 512], BF16) for i in range(2)]
    R_b = sb("R", [128, 2, 512], BF16)
    ec_b = [sb("ec%d" % i, [128, 2, 512], F32) for i in range(2)]
    a_b = [sb("a%d" % i, [128, 2, 512], BF16) for i in range(2)]
    endA2 = off[0]
    assert endA2 <= OV + 30720
    off[0] = OV
    wst = [sb("wst%d" % i, [128, 1280], F32) for i in range(4)]
    off[0] = PH
    w_up_sb = sb("w_up_sb", [128, 8, 2 * DFF], BF16)
    w_down_sb = sb("w_down_sb", [128, 22, D], BF16)
    hring = [sb("hring%d" % i, [128, D], F32) for i in range(2)]
    xnb4B = sb("xnb4B", [128, 4, D], BF16)
    hnT = sb("hnT", [128, 8, 512], BF16)
    actT = sb("actT", [128, 22, 512], BF16)
    wstB = [sb("wstB%d" % i, [128, 1280], F32, at=off[0] - 22 * 1024 + i * 5120) for i in range(4)]
    upb = [sb("upb%d" % i, [128, 520], F32) for i in range(3)]
    tba = [sb("tba%d" % i, [128, 512], F32) for i in range(1)]
    tbg = [sb("tbg%d" % i, [128, 512], F32) for i in range(3)]
    yst = [sb("yst%d" % i, [128, D], F32) for i in range(2)]
    assert max(endA1, endA2, off[0]) <= SB_END

    PS = [nc.alloc_psum_tensor("ps%d" % i, [128, 2, 512], F32) for i in range(4)]

    def bank(i):
        return PS[i // 2][:, i % 2, :]

    P_ = Prog()
    add = P_.add
    alias = {}
    for nm in (['sq0', 'sq1', 'tq0', 'tq1', 'kst0', 'kst1', 'vst0', 'vst1', 'sg0', 'sg1', 'lnm', 'lnr', 'lnt0', 'lnt1'] +
               ['e0', 'e1', 'e2', 'sp0', 'sp1', 'R', 'ec0', 'ec1', 'a0', 'a1']):
        at_, nb_ = sb_map[nm]
        assert OV <= at_ and at_ + nb_ <= OV + 30720, nm
        alias[nm] = ['pg%d' % p for p in range((at_ - OV) // 2048, (at_ + nb_ - 1 - OV) // 2048 + 1)]
    P_.alias = alias

    def mm(out, lhsT, rhs, start, stop, r, w):
        add('pe', lambda e: e.matmul(out, lhsT, rhs, start=start, stop=stop), r=r, w=w)

    def tr(out, in_, idn, r, w):
        add('pe', lambda e: e.transpose(out, in_, idn), r=r, w=w)

    def act(out, in_, func, r, w, bias=0.0, scale=1.0):
        add('act', lambda e: e.activation(out, in_, func, bias=bias, scale=scale), r=r, w=w)

    def tt(eng, out, in0, in1, op, r, w):
        add(eng, lambda e: e.tensor_tensor(out, in0, in1, op), r=r, w=w)

    def ts(eng, out, in0, s1, s2, op0, op1, r, w):
        add(eng, lambda e: e.tensor_scalar(out, in0, s1, s2, op0, op1), r=r, w=w)

    def stt(out, in0, scalar, in1, op0, op1, r, w):
        add('dve', lambda e: e.scalar_tensor_tensor(out, in0, scalar, in1, op0, op1), r=r, w=w)

    def cp(eng, out, in_, r, w):
        if eng == 'act':
            add('act', lambda e: e.activation(out, in_, AF.Copy), r=r, w=w)
        else:
            add(eng, lambda e: e.tensor_copy(out, in_), r=r, w=w)

    def mset(eng, ap, val, w):
        add(eng, lambda e: e.memset(ap, val), w=w)

    def rsum(out, in_, r, w):
        add('dve', lambda e: e.reduce_sum(out, in_, AX.X), r=r, w=w)

    def dma(out, in_, key, r=(), w=()):
        add('sp', lambda e: e.dma_start(out, in_), r=r, w=w, dma=key)

    def rstd_from(ssap, outap, inv_n, key):
        act(outap, ssap, AF.Ln, r=[key], w=[key], bias=EPS, scale=inv_n)
        act(outap, outap, AF.Exp, r=[key], w=[key], scale=-0.5)

    rot = [0]

    def nbank(lo, hi):
        b = lo + rot[0] % (hi - lo)
        rot[0] += 1
        return b

    dma(c_f32[:], consts, 'setup', w=['c_f32'])
    dma(smp[:], sm, 'setup2', w=['smp'])
    ts('dve', Lb[:], c_f32[:, 1, :], -1.0, None, ALU.mult, ALU.bypass, r=['c_f32'], w=['Lb'])
    ts('dve', onesnb[:], c_f32[:, 2, :], -1.0, None, ALU.mult, ALU.bypass, r=['c_f32'], w=['onesnb'])
    ts('dve', smp[:, 16:80], smp[:, 16:80], 0.125, None, ALU.mult, ALU.bypass, r=['smp'], w=['smp'])
    cp('dve', onesb[:], c_f32[:, 2, :], r=['c_f32'], w=['onesb'])
    mset('pool', zerosb[:], 0.0, w=['zerosb'])
    cp('dve', identb[:], c_f32[:, 0, :], r=['c_f32'], w=['identb'])
    ts('dve', negmb[:], c_f32[:, 3, :], -1.0, 30000.0, ALU.add, ALU.mult, r=['c_f32'], w=['negmb'])
    mset('pool', stat[:, 63:64], -0.5, w=['mhalf'])
    cp('dve', mask2[:, 0, :], c_f32[:, 3, :], r=['c_f32'], w=['mask2a'])
    cp('dve', mask2[:, 1, :], c_f32[:, 3, :], r=['c_f32'], w=['mask2b'])
    P_.barrier()

    def load_weight(dst, src, nk, ncols, scale, stg, tag):
        cnt = 0
        for kc in range(nk):
            for c0 in range(0, ncols, 1280):
                cw = min(1280, ncols - c0)
                j = cnt % len(stg)
                cnt += 1
                s = stg[j]
                dma(s[:, 0:cw], src[kc * 128:(kc + 1) * 128, c0:c0 + cw], tag + str(j), w=[tag + str(j)])
                if cnt % 3 != 0:
                    if scale is not None:
                        ts('dve', dst[:, kc, c0:c0 + cw], s[:, 0:cw], scale[:, kc:kc + 1], None, ALU.mult, ALU.bypass,
                           r=[tag + str(j)], w=[tag + 'dst'])
                    else:
                        cp('dve', dst[:, kc, c0:c0 + cw], s[:, 0:cw], r=[tag + str(j)], w=[tag + 'dst'])
                else:
                    if scale is not None:
                        act(dst[:, kc, c0:c0 + cw], s[:, 0:cw], AF.Identity, r=[tag + str(j)], w=[tag + 'dst'],
                            scale=scale[:, kc:kc + 1])
                    else:
                        cp('act', dst[:, kc, c0:c0 + cw], s[:, 0:cw], r=[tag + str(j)], w=[tag + 'dst'])

    load_weight(w_in_sb, w_in, 8, IN, gmix, wst, 'wst')
    load_weight(w_out_sb, w_out, 8, D, None, wst, 'wst')
    P_.barrier()

    def norm_dma(src, row0, s, tp, ring, pfx):
        j = s % len(ring)
        rk = pfx + 'ring%d' % j
        dma(ring[j][0:tp, :], src[row0 + s * tp: row0 + (s + 1) * tp, :], rk, w=[rk])

    def norm_cmp(s, tp, ring, xnb4, pfx):
        j = s % len(ring)
        rb = ring[j]
        rk = pfx + 'ring%d' % j
        xk = pfx + 'xn%d' % s
        xb = xnb4[0:tp, s, :]
        act(xb, rb[0:tp, :], AF.Square, r=[rk], w=[xk])
        ssap = stat[0:tp, s:s + 1]
        skey = pfx + 'ss%d' % s
        rsum(ssap, xb, r=[xk], w=[skey])
        ts('pool', ssap, ssap, 1.0 / D, EPS, ALU.mult, ALU.add, r=[skey], w=[skey])
        tt('pool', ssap, ssap, stat[0:tp, 63:64], ALU.pow, r=[skey, 'mhalf'], w=[skey])
        ts('dve', xb, rb[0:tp, :], ssap, None, ALU.mult, ALU.bypass, r=[rk, skey, xk], w=[xk])

    def norm_chain(src, row0, s, tp, ring, xnb4, pfx):
        norm_dma(src, row0, s, tp, ring, pfx)
        norm_cmp(s, tp, ring, xnb4, pfx)

    def norm_tr(s, tp, xnb4, outT, pfx):
        xk = pfx + 'xn%d' % s
        for b in range(2):
            bkb = bank(b).bitcast(BF16)
            for q in range(4):
                kc = 4 * b + q
                tr(bkb[:, q * 128: q * 128 + tp], xnb4[0:tp, s, kc * 128:(kc + 1) * 128], identb[0:tp, 0:tp],
                   r=[xk, 'identb'], w=['bank%d' % b])
            src_ap = bkb[:, 0:512].rearrange("p (a b) -> p a b", b=128)[:, :, 0:tp]
            cp('act' if (b == 0 or pfx == 'A') else 'dve', outT[:, 4 * b:4 * b + 4, s * tp:(s + 1) * tp], src_ap,
               r=['bank%d' % b], w=[pfx + 'outT'])

    hn_cnt = [0]

    def head_norm(bk, bkey, tp, gvec, dst, dkey):
        jj = hn_cnt[0] % 2
        hn_cnt[0] += 1
        sq = sq2[jj]
        tq = tq2[jj]
        sqk = 'sq%d' % jj
        tqk = 'tq%d' % jj
        s8k = 'ss8%d' % jj
        act(sq[0:tp, :], bk[0:tp, :], AF.Square, r=[bkey], w=[sqk])
        ss8 = stat[0:tp, 8 + 8 * jj:16 + 8 * jj]
        rsum(ss8, sq[0:tp, :].rearrange("p (h d) -> p h d", d=64), r=[sqk], w=[s8k])
        rstd_from(ss8, ss8, 1.0 / 64, s8k)
        tt('dve', tq[0:tp, :].rearrange("p (h d) -> p h d", d=64), bk[0:tp, :].rearrange("p (h d) -> p h d", d=64),
           ss8.unsqueeze(2).to_broadcast([tp, 8, 64]), ALU.mult, r=[bkey, s8k], w=[tqk])
        tt('pool', dst[0:tp, :].rearrange("p (h d) -> p h d", d=64), tq[0:tp, :].rearrange("p (h d) -> p h d", d=64),
           gvec[0:tp, :].unsqueeze(1).to_broadcast([tp, 8, 64]), ALU.mult, r=[tqk, 'smp'], w=[dkey])

    def inproj(x_src, row0, ntok, nsub, tp, kcol0, vblk0, kout, vout, first_tile, conv_in_state, inter):
        for s_ in range(nsub):
            norm_tr(s_, tp, xnb4A, xnT, 'A')
        stc = [0]

        def group(s, g):
            b = nbank(2, 8)
            bk = bank(b)
            bkey = 'bank%d' % b
            for kc in range(8):
                mm(bk[0:tp, :], xnT[:, kc, s * tp:(s + 1) * tp], w_in_sb[:, kc, g * 512:(g + 1) * 512],
                   kc == 0, kc == 7, r=['AoutT', 'w_in'], w=[bkey])
            j = stc[0] % 2
            stc[0] += 1
            if g < 2:
                st_ = kst[j]
                skey = 'kst%d' % j
                head_norm(bk, bkey, tp, gq if g == 0 else gk, st_, skey)
                if g == 1:
                    dma(kout[row0 + s * tp: row0 + (s + 1) * tp, :], st_[0:tp, :], skey, r=[skey])

                def post():
                    b2 = nbank(2, 8)
                    bk2 = bank(b2)
                    for pr in range(4):
                        tr(bk2[:, pr * 128: pr * 128 + tp], st_[0:tp, pr * 128:(pr + 1) * 128], ident[0:tp, 0:tp],
                           r=[skey, 'c_f32'], w=['bank%d' % b2])
                    src_ap = bk2.rearrange("p (a b) -> p a b", b=128)[:, :, 0:tp]
                    if g == 0:
                        cp('act', qT[:, :, s * tp:(s + 1) * tp], src_ap, r=['bank%d' % b2], w=['qT'])
                    else:
                        c0 = kcol0 + s * tp
                        cp('act', kT[:, :, c0:c0 + tp], src_ap, r=['bank%d' % b2], w=['kT'])
                return post
            else:
                st_ = vst[j]
                skey = 'vst%d' % j
                cp('act', st_[0:tp, :], bk[0:tp, :], r=[bkey], w=[skey])
                dma(vout[row0 + s * tp: row0 + (s + 1) * tp, :], st_[0:tp, :], skey, r=[skey])
                cp('pool', v_sb[0:tp, vblk0 + s, :], st_[0:tp, :], r=[skey], w=['v_sb'])
                return None

        def fchunk(c):
            bu = nbank(2, 8)
            bg = nbank(2, 8)
            for kc in range(8):
                mm(bank(bg)[:, 0:ntok], w_in_sb[:, kc, 2048 + c * 128: 2048 + (c + 1) * 128], xnT[:, kc, 0:ntok],
                   kc == 0, kc == 7, r=['AoutT', 'w_in'], w=['bank%d' % bg])
            j = c % 2
            act(sg[j][:, 0:ntok], bank(bg)[:, 0:ntok], AF.Sigmoid, r=['bank%d' % bg], w=['sg%d' % j])
            for kc in range(8):
                mm(bank(bu)[:, 0:ntok], w_in_sb[:, kc, 1536 + c * 128: 1536 + (c + 1) * 128], xnT[:, kc, 0:ntok],
                   kc == 0, kc == 7, r=['AoutT', 'w_in'], w=['bank%d' % bu])
            tt('dve', cbuf[:, c, 30:30 + ntok], bank(bu)[:, 0:ntok], sg[j][:, 0:ntok], ALU.mult,
               r=['bank%d' % bu, 'sg%d' % j], w=['cbuf_c%d' % c])

        groups = [(s, g) for s in range(nsub) for g in range(3)]
        pend = None
        ii = 0
        for idx, (s, g) in enumerate(groups):
            post = group(s, g)
            for _ in range(3):
                if ii < len(inter):
                    inter[ii]()
                    ii += 1
            if pend is not None:
                pend()
            pend = post
        if pend is not None:
            pend()
        while ii < len(inter):
            inter[ii]()
            ii += 1

        def part2():
            if first_tile:
                if conv_in_state is None:
                    mset('pool', cbuf[:, :, 0:30], 0.0, w=['cbuf_h'])
                else:
                    dma(cbuf[:, :, 0:30], conv_in_state, 'cbufin', w=['cbuf_h'])
            for c in range(4):
                fchunk(c)
        return part2

    def conv_ops(ntok):
        ops = []

        def mk(tau, c):
            def f():
                if tau == 0:
                    ts('dve', cvb[:, c, 0:ntok], cbuf[:, c, 0:ntok], wdw[:, c, 0:1], bdw[:, c:c + 1], ALU.mult, ALU.add,
                       r=['cbuf_c%d' % c, 'cbuf_h', 'smp'], w=['cv%d' % c])
                else:
                    stt(cvb[:, c, 0:ntok], cbuf[:, c, tau:tau + ntok], wdw[:, c, tau:tau + 1], cvb[:, c, 0:ntok],
                        ALU.mult, ALU.add, r=['cbuf_c%d' % c, 'cbuf_h', 'cv%d' % c], w=['cv%d' % c])
            return f
        for tau in range(31):
            for c in range(4):
                ops.append(mk(tau, c))
        return ops

    def conv_ln(ntok):
        b1 = nbank(2, 8)
        b2 = nbank(2, 8)
        for c in range(4):
            j = c % 2
            v16 = sg[j][:].bitcast(BF16)
            cp('act', v16[:, 0:ntok], cvb[:, c, 0:ntok], r=['cv%d' % c], w=['sg%d' % j])
            act(v16[:, 512:512 + ntok], cvb[:, c, 0:ntok], AF.Square, r=['cv%d' % c], w=['sg%d' % j])
            mm(bank(b1)[:, 0:ntok], onesb[:], v16[:, 0:ntok], c == 0, c == 3, r=['sg%d' % j, 'onesb'], w=['bank%d' % b1])
            mm(bank(b2)[:, 0:ntok], onesb[:], v16[:, 512:512 + ntok], c == 0, c == 3, r=['sg%d' % j, 'onesb'], w=['bank%d' % b2])
        ts('dve', lnm[:, 0:ntok], bank(b1)[:, 0:ntok], 1.0 / 512, None, ALU.mult, ALU.bypass, r=['bank%d' % b1], w=['lnm'])
        tt('dve', lnt[0][:, 0:ntok], lnm[:, 0:ntok], lnm[:, 0:ntok], ALU.mult, r=['lnm'], w=['lnt0'])
        stt(lnr[:, 0:ntok], bank(b2)[:, 0:ntok], 1.0 / 512, lnt[0][:, 0:ntok], ALU.mult, ALU.subtract,
            r=['bank%d' % b2, 'lnt0'], w=['lnr'])
        ts('dve', lnr[:, 0:ntok], lnr[:, 0:ntok], 0.0, None, ALU.max, ALU.bypass, r=['lnr'], w=['lnr'])
        act(lnr[:, 0:ntok], lnr[:, 0:ntok], AF.Ln, r=['lnr'], w=['lnr'], bias=EPS)
        act(lnr[:, 0:ntok], lnr[:, 0:ntok], AF.Exp, r=['lnr'], w=['lnr'], scale=-0.5)
        for c in range(4):
            j = c % 2
            tt('dve', lnt[j][:, 0:ntok], cvb[:, c, 0:ntok], lnm[:, 0:ntok], ALU.subtract, r=['cv%d' % c, 'lnm'], w=['lnt%d' % j])
            tt('dve', lnt[j][:, 0:ntok], lnt[j][:, 0:ntok], lnr[:, 0:ntok], ALU.mult, r=['lnt%d' % j, 'lnr'], w=['lnt%d' % j])
            act(ccT[:, c, 0:ntok], lnt[j][:, 0:ntok], AF.Silu, r=['lnt%d' % j, 'smp'], w=['ccT'],
                bias=bln[:, c:c + 1], scale=gln[:, c:c + 1])

    def attention(N, blocks, extra, tail=(), head=()):
        steps = []
        for pr in range(4):
            for bi, (kb, c0, dg) in enumerate(blocks):
                steps.append((pr, kb, c0, dg, bi == 0, bi == len(blocks) - 1))
        n = len(steps)

        def stageA(m):
            pr, kb, c0, dg, first, last = steps[m]
            Z = PS[0]
            zk = 'Zb'
            e = e_b[m % 3]
            ek = 'e%d' % (m % 3)
            for h in range(2):
                mm(Z[:, h, c0:N], kT[64 * h:64 * h + 64, pr, kb * 128:(kb + 1) * 128], qT[64 * h:64 * h + 64, pr, c0:N],
                   True, not dg, r=['kT', 'qT'], w=['bank0', 'bank1'])
            if dg:
                mw = min(128, N - c0)
                for h in range(2):
                    mm(Z[:, h, c0:c0 + mw], identb[:], negmb[:, 0:mw], False, True, r=['identb', 'negmb'], w=['bank0', 'bank1'])
            act(Z[:, :, c0:N], Z[:, :, c0:N], AF.Exp, r=['bank0', 'bank1'], w=['bank0', 'bank1'])
            act(sp_b[m % 2][:, :, c0:N], Z[:, :, c0:N], AF.Ln, r=['bank0', 'bank1'], w=['sp%d' % (m % 2)], bias=1.0)

        def stageB(m):
            pr, kb, c0, dg, first, last = steps[m]
            C = PS[1 + m % 2]
            ckl = ['bank2', 'bank3'] if m % 2 == 0 else ['bank4', 'bank5']
            spk = 'sp%d' % (m % 2)
            spt = sp_b[m % 2]
            if first:
                mset('pool', R_b[:, :, 0:N], 0.0, w=['R'])
            for h in range(2):
                mm(C[:, h, c0:N], Lb[:], spt[:, h, c0:N], True, False, r=[spk, 'Lb'], w=ckl)
                if not first:
                    mm(C[:, h, c0:N], onesnb[:], R_b[:, h, c0:N], False, False, r=['R', 'onesnb'], w=ckl)
            for h in range(2):
                mm(C[:, h, c0:N], kT[64 * h:64 * h + 64, pr, kb * 128:(kb + 1) * 128], qT[64 * h:64 * h + 64, pr, c0:N],
                   False, not dg, r=['kT', 'qT'], w=ckl)
            if dg:
                mw = min(128, N - c0)
                for h in range(2):
                    mm(C[:, h, c0:c0 + mw], identb[:], negmb[:, 0:mw], False, True, r=['identb', 'negmb'], w=ckl)
            if not last:
                tt('dve', R_b[:, :, c0:N], R_b[:, :, c0:N], spt[:, :, c0:N], ALU.add, r=[spk, 'R'], w=['R'])
            act(a_b[m % 2][:, :, c0:N], C[:, :, c0:N], AF.Exp, r=ckl, w=['a%d' % (m % 2)])

        def stageC(m):
            pr, kb, c0, dg, first, last = steps[m]
            O = PS[3][:, 0, :]
            ok = 'bank6'
            if first:
                for h in range(2):
                    mm(O[64 * h:64 * h + 64, 0:N], zerosb[:, 0:64], w_out_sb[:, 0, 0:N], True, False,
                       r=['zerosb', 'w_out'], w=[ok])
            for h in range(2):
                mm(O[64 * h:64 * h + 64, c0:N], v_sb[:, kb, (2 * pr + h) * 64:(2 * pr + h + 1) * 64], a_b[m % 2][:, h, c0:N],
                   False, last, r=['v_sb', 'a%d' % (m % 2)], w=[ok])
            if last:
                cp('dve', attnT[:, pr, 0:N], O[:, 0:N], r=[ok], w=['attnT'])

        extra = list(extra)
        xi = 0
        hstride = max(1, min(3, (len(blocks) - 1) // max(1, len(head)))) if head else 1
        per_step = max(1, min(4, -(-len(extra) // (n + 1))))
        for it in range(n + 2):
            if it < n:
                stageA(it)
            if 0 <= it - 1 < n:
                stageB(it - 1)
            if head and it % hstride == 0 and it // hstride < len(head):
                head[it // hstride]()
            for _ in range(per_step):
                if xi < len(extra):
                    extra[xi]()
                    xi += 1
            if 0 <= it - 2 < n:
                stageC(it - 2)
            t0_ = max(0, n - 3 * len(tail) - 2)
            if it >= t0_ and (it - t0_) % 3 == 0 and (it - t0_) // 3 < len(tail):
                tail[(it - t0_) // 3]()
        for i_ in range(len(tail)):
            if n + 2 <= t0_ + 3 * i_:
                tail[i_]()
        return extra[xi:]

    def outproj(x_src, row0, hrow0, nsub, tp, deferred=False):
        stg = [(kst[0], 'kst0'), (kst[1], 'kst1'), (vst[0], 'vst0'), (vst[1], 'vst1')]
        items = [(s, half) for s in range(nsub) for half in range(2)]
        if deferred:
            def mk(i, s, half):
                def f():
                    sbuf_, hk = stg[i % 2]
                    cs = slice(half * 512, (half + 1) * 512)
                    dma(sbuf_[0:tp, :], x_src[row0 + s * tp: row0 + (s + 1) * tp, cs], hk, w=[hk])
                    bk = bank(7)
                    for kc in range(8):
                        src = attnT if kc < 4 else ccT
                        mm(bk[0:tp, :], src[:, kc % 4, s * tp:(s + 1) * tp], w_out_sb[:, kc, cs],
                           kc == 0, kc == 7, r=['attnT', 'ccT', 'w_out'], w=['bank7'])
                    tt('dve', sbuf_[0:tp, :], bk[0:tp, :], sbuf_[0:tp, :], ALU.add, r=['bank7', hk], w=[hk])
                    dma(hscr[hrow0 + s * tp: hrow0 + (s + 1) * tp, cs], sbuf_[0:tp, :], hk, r=[hk])
                return f
            return [mk(i, s, half) for i, (s, half) in enumerate(items)]

        def ld(i):
            s, half = items[i]
            sbuf_, hk = stg[i % 4]
            dma(sbuf_[0:tp, :], x_src[row0 + s * tp: row0 + (s + 1) * tp, half * 512:(half + 1) * 512], hk, w=[hk])
        for i in range(min(4, len(items))):
            ld(i)
        for i, (s, half) in enumerate(items):
            if True:
                sbuf_, hk = stg[i % 4]
                cs = slice(half * 512, (half + 1) * 512)
                b = nbank(0, 8)
                bk = bank(b)
                for kc in range(8):
                    src = attnT if kc < 4 else ccT
                    mm(bk[0:tp, :], src[:, kc % 4, s * tp:(s + 1) * tp], w_out_sb[:, kc, cs],
                       kc == 0, kc == 7, r=['attnT', 'ccT', 'w_out'], w=['bank%d' % b])
                tt('dve', sbuf_[0:tp, :], bk[0:tp, :], sbuf_[0:tp, :], ALU.add, r=['bank%d' % b, hk], w=[hk])
                dma(hscr[hrow0 + s * tp: hrow0 + (s + 1) * tp, cs], sbuf_[0:tp, :], hk, r=[hk])
                if i + 4 < len(items):
                    ld(i + 4)

    seqs = []
    for sidx in range(NSEQ):
        seqs.append(dict(x=xp, row0=sidx * T, hrow0=sidx * T, T=T, tile=512, tp=128, kout=kp, vout=vp,
                         cst=cstp[sidx], fst=fstp[sidx], yout=yp, past=0, conv_state=None, ffn_state=None))
    seqs.append(dict(x=xs, row0=0, hrow0=NTOK, T=TS, tile=TS, tp=TS, kout=ks, vout=vs,
                     cst=csts, fst=fsts, yout=ys, past=NPB, conv_state=sconvT, ffn_state=sffnT))

    tilesA = []
    for sq_ in seqs:
        for ti in range(sq_['T'] // sq_['tile']):
            tilesA.append((sq_, ti))

    def chainA(k):
        sq_, ti = tilesA[k]
        for s_ in range(sq_['tile'] // sq_['tp']):
            norm_chain(sq_['x'], sq_['row0'] + ti * sq_['tile'], s_, sq_['tp'], xring, xnb4A, 'A')

    def seq_start(sq_):
        past = sq_['past']
        if past:
            for blk in range(past):
                j = blk % 2
                dma(kst[j][:].rearrange("p (a b) -> p a b", b=128), ckT[:, :, blk * 128:(blk + 1) * 128], 'kst%d' % j, w=['kst%d' % j])
                cp('dve', kT[:, :, blk * 128:(blk + 1) * 128], kst[j][:].rearrange("p (a b) -> p a b", b=128), r=['kst%d' % j], w=['kT'])
                dma(vst[j][:], cv[blk * 128:(blk + 1) * 128, :], 'vst%d' % j, w=['vst%d' % j])
                cp('act', v_sb[:, blk, :], vst[j][:], r=['vst%d' % j], w=['v_sb'])
            mset('pool', kT[:, :, past * 128:(past + 1) * 128], 0.0, w=['kT'])
            mset('pool', v_sb[:, past, :], 0.0, w=['v_sb'])

    def tile_params(k):
        sq_, ti = tilesA[k]
        ntok = sq_['tile']
        tp = sq_['tp']
        past = sq_['past']
        return dict(sq=sq_, ti=ti, ntile=sq_['T'] // ntok, ntok=ntok, tp=tp, nsub=ntok // tp, past=past,
                    row0=sq_['row0'] + ti * ntok, kcol0=past * 128 + ti * ntok,
                    vblk0=past + ti * (ntok // 128 if tp == 128 else 0))

    def do_inproj(k, inter):
        t = tile_params(k)
        sq_ = t['sq']
        if t['ti'] == 0:
            seq_start(sq_)
        return inproj(sq_['x'], t['row0'], t['ntok'], t['nsub'], t['tp'], t['kcol0'], t['vblk0'], sq_['kout'],
                      sq_['vout'], t['ti'] == 0, sq_['conv_state'], inter)

    chainA(0)
    p2 = do_inproj(0, [])
    p2()
    head_cur = []
    for kA in range(len(tilesA)):
        t = tile_params(kA)
        sq_, ti, ntok, tp, nsub, past = t['sq'], t['ti'], t['ntok'], t['tp'], t['nsub'], t['past']
        blocks = []
        if tp == 128:
            for j in range(3, -1, -1):
                blocks.append((past + ti * 4 + j, 128 * j, True))
            for kb in range(past + ti * 4 - 1, -1, -1):
                blocks.append((kb, 0, False))
        else:
            blocks.append((past, 0, True))
            for kb in range(past - 1, -1, -1):
                blocks.append((kb, 0, False))
        tail = []
        if kA + 1 < len(tilesA):
            nsq, nti = tilesA[kA + 1]
            nns = nsq['tile'] // nsq['tp']
            nrow0 = nsq['row0'] + nti * nsq['tile']

            def mk_tail(i_):
                def f():
                    if i_ >= 1:
                        norm_cmp(i_ - 1, nsq['tp'], xring, xnb4A, 'A')
                    if i_ < nns:
                        norm_dma(nsq['x'], nrow0, i_, nsq['tp'], xring, 'A')
                return f
            tail = [mk_tail(i_) for i_ in range(nns + 1)]
        left = attention(ntok, blocks, conv_ops(ntok), tail, head_cur)
        head_cur = []
        p2 = None
        if kA + 1 < len(tilesA):
            p2 = do_inproj(kA + 1, left)
        else:
            for f_ in left:
                f_()
        conv_ln(ntok)
        if ti == t['ntile'] - 1:
            dma(sq_['cst'], cbuf[:, :, ntok:ntok + 30], 'cbufout', r=['cbuf_c0', 'cbuf_c1', 'cbuf_c2', 'cbuf_c3', 'cbuf_h'])
        else:
            cp('pool', cbuf[:, :, 0:30], cbuf[:, :, ntok:ntok + 30], r=['cbuf_c0', 'cbuf_c1', 'cbuf_c2', 'cbuf_c3', 'cv0', 'cv1', 'cv2', 'cv3'], w=['cbuf_h'])
        if p2 is not None:
            p2()
        if kA + 1 < len(tilesA) and tilesA[kA + 1][1] != 0:
            head_cur = outproj(sq_['x'], t['row0'], sq_['hrow0'] + ti * ntok, nsub, tp, deferred=True)
        else:
            outproj(sq_['x'], t['row0'], sq_['hrow0'] + ti * ntok, nsub, tp)
    P_.barrier()

    load_weight(w_up_sb, w_up, 8, 2 * DFF, gffn, wstB, 'wstB')
    load_weight(w_down_sb, w_down, 22, D, None, wstB, 'wstB')
    P_.barrier()

    tilesB = []
    for sq_ in seqs:
        for ti in range(sq_['T'] // sq_['tile']):
            tilesB.append((sq_, ti))
    for kB, (sq_, ti) in enumerate(tilesB):
        ntile = sq_['T'] // sq_['tile']
        ntok = sq_['tile']
        tp = sq_['tp']
        nsub = ntok // tp
        hrow0 = sq_['hrow0'] + ti * ntok
        if kB + 1 < len(tilesB):
            nsq, nti = tilesB[kB + 1]
            nxt = (nsq['hrow0'] + nti * nsq['tile'], nsq['tp'], nsq['tile'] // nsq['tp'])
        else:
            nxt = None
        if kB == 0:
            for s_ in range(nsub):
                norm_chain(hscr, hrow0, s_, tp, hring, xnb4B, 'B')
            for s_ in range(nsub):
                norm_tr(s_, tp, xnb4B, hnT, 'B')
        if ti == 0:
            if sq_['ffn_state'] is None:
                mset('pool', halo[:], 0.0, w=['halo%d' % c_ for c_ in range(NFC)])
            else:
                dma(halo[:], sq_['ffn_state'], 'haloin', w=['halo%d' % c_ for c_ in range(NFC)])
        if True:
            ucnt = 0
            pendB = None

            def silu_mult(t_, tk, m):
                act(t_[:, 0:ntok], t_[:, 0:ntok], AF.Silu, r=[tk], w=[tk])
                tt('pool', actT[:, m, 0:ntok], t_[:, 0:ntok], actT[:, m, 0:ntok], ALU.mult,
                   r=[tk, 'actT%d' % m], w=['actT%d' % m])

            for m in range(22):
                newp = None
                for ch in (m, m + 22):
                    b = nbank(2, 8)
                    bk = bank(b)
                    bkey = 'bank%d' % b
                    for kc in range(8):
                        mm(bk[:, 0:ntok], w_up_sb[:, kc, ch * 128:(ch + 1) * 128], hnT[:, kc, 0:ntok], kc == 0, kc == 7,
                           r=['BoutT', 'w_up'], w=[bkey])
                    ju = ucnt % 3
                    ucnt += 1
                    ub = upb[ju]
                    uk = 'upb%d' % ju
                    if ch == m:
                        t_ = tba[0]
                        tk = 'tba0'
                    else:
                        t_ = tbg[m % 3]
                        tk = "tbg%d" % (m % 3)
                    cp('dve', ub[:, 0:2], halo[:, ch, :], r=['halo%d' % ch], w=[uk + 'h'])
                    cp('act', ub[:, 2:2 + ntok], bk[:, 0:ntok], r=[bkey], w=[uk])
                    act(t_[:, 0:ntok], bk[:, 0:ntok], AF.Identity, r=[bkey, 'smp'], w=[tk],
                        bias=bfd[:, ch:ch + 1], scale=wfd[:, ch, 2:3])
                    cp('dve', halo[:, ch, :], ub[:, ntok:ntok + 2], r=[uk, uk + 'h'], w=['halo%d' % ch])
                    stt(t_[:, 0:ntok], ub[:, 1:1 + ntok], wfd[:, ch, 1:2], t_[:, 0:ntok], ALU.mult, ALU.add,
                        r=[uk, uk + 'h', tk], w=[tk])
                    if ch == m:
                        stt(actT[:, m, 0:ntok], ub[:, 0:ntok], wfd[:, ch, 0:1], t_[:, 0:ntok], ALU.mult, ALU.add,
                            r=[uk, uk + 'h', tk], w=['actT%d' % m])
                    else:
                        stt(t_[:, 0:ntok], ub[:, 0:ntok], wfd[:, ch, 0:1], t_[:, 0:ntok], ALU.mult, ALU.add,
                            r=[uk, uk + 'h', tk], w=[tk])
                        newp = (t_, tk, m)
                if pendB is not None:
                    silu_mult(*pendB)
                pendB = newp
                if nxt is not None and m in (1, 6, 11, 16):
                    s_n = (1, 6, 11, 16).index(m)
                    if s_n < nxt[2]:
                        norm_dma(hscr, nxt[0], s_n, nxt[1], hring, 'B')
                if nxt is not None and m in (3, 8, 13, 18):
                    s_n = (3, 8, 13, 18).index(m)
                    if s_n < nxt[2]:
                        norm_cmp(s_n, nxt[1], hring, xnb4B, 'B')
            silu_mult(*pendB)
            if ti == ntile - 1:
                dma(sq_['fst'], halo[:], 'haloout', r=['halo%d' % c_ for c_ in range(NFC)])
            for s in range(nsub):
                if nxt is not None and s < nxt[2]:
                    norm_tr(s, nxt[1], xnb4B, hnT, 'B')
                j = s % 2
                yk = 'yst%d' % j
                dma(yst[j][0:tp, :], hscr[hrow0 + s * tp: hrow0 + (s + 1) * tp, :], yk, w=[yk])
                for half in range(2):
                    b = nbank(2, 8)
                    bk = bank(b)
                    for m in range(22):
                        mm(bk[0:tp, :], actT[:, m, s * tp:(s + 1) * tp], w_down_sb[:, m, half * 512:(half + 1) * 512],
                           m == 0, m == 21, r=['actT%d' % m, 'w_down'], w=['bank%d' % b])
                    tt('dve', yst[j][0:tp, half * 512:(half + 1) * 512], bk[0:tp, :], yst[j][0:tp, half * 512:(half + 1) * 512],
                       ALU.add, r=['bank%d' % b, yk], w=[yk])
                orow = sq_['row0'] + ti * ntok + s * tp
                dma(sq_['yout'][orow: orow + tp, :], yst[j][0:tp, :], yk, r=[yk])
    P_.barrier()

    stack = contextlib.ExitStack()
    with stack:
        P_.emit(nc, stack)
    return nc


def _consts():
    c = np.zeros((128, 4, 128), np.float32)
    i = np.arange(128)
    c[:, 0, :] = np.eye(128, dtype=np.float32)
    c[:, 1, :] = (i[:, None] >= i[None, :]).astype(np.float32)
    c[:, 2, :] = 1.0
    c[:, 3, :] = (i[None, :] > i[:, None]).astype(np.float32)
    return c


def _small(g_mix, g_ffn, g_q, g_k, w_dw, b_dw, g_ln, b_ln, w_fd, b_fd):
    sm = np.zeros((128, 512), np.float32)
    sm[:, 0:8] = g_mix.reshape(8, 128).T
    sm[:, 8:16] = g_ffn.reshape(8, 128).T
    sm[:, 16:80] = g_q[None, :]
    sm[:, 80:144] = g_k[None, :]
    sm[:, 144:268] = w_dw.reshape(31, 4, 128).transpose(2, 1, 0).reshape(128, 124)
    sm[:, 268:272] = b_dw.reshape(4, 128).T
    sm[:, 272:276] = g_ln.reshape(4, 128).T
    sm[:, 276:280] = b_ln.reshape(4, 128).T
    sm[:, 280:412] = w_fd.reshape(3, NFC, 128).transpose(2, 1, 0).reshape(128, 132)
    sm[:, 412:456] = b_fd.reshape(NFC, 128).T
    return sm


_NC_CACHE = {}


def run_cores(x_prompt, x_sample, cache_sb_k, cache_sb_v, state_conv, state_ffn_conv,
              g_mix, w_in, g_q, g_k, w_dw, b_dw, g_conv_ln, b_conv_ln, w_out,
              g_ffn, w_up, w_ffn_dw, b_ffn_dw, w_down, n_cores=8):
    f = lambda a: np.ascontiguousarray(np.asarray(a, dtype=np.float32))
    x_prompt = f(x_prompt); x_sample = f(x_sample)
    B, T, _ = x_prompt.shape
    NSEQ = B // n_cores
    P = cache_sb_k.shape[2]
    key = (NSEQ, T, P)
    if key not in _NC_CACHE:
        _NC_CACHE[key] = build_nc(NSEQ, T, P)
    nc = _NC_CACHE[key]
    consts = _consts()
    sm = _small(f(g_mix)[0], f(g_ffn)[0], f(g_q)[0], f(g_k)[0], f(w_dw)[0], f(b_dw)[0], f(g_conv_ln)[0],
                f(b_conv_ln)[0], f(w_ffn_dw)[0], f(b_ffn_dw)[0])
    wi = f(w_in)[0]; wo = f(w_out)[0]; wu = f(w_up)[0]; wd = f(w_down)[0]
    ck = f(cache_sb_k)[0]; cvv = f(cache_sb_v)[0]; sc = f(state_conv)[0]; sf = f(state_ffn_conv)[0]
    in_maps = []
    for c in range(n_cores):
        ckT = np.ascontiguousarray(ck[c].reshape(P, 4, 128).transpose(2, 1, 0))
        in_maps.append(dict(
            xp=np.ascontiguousarray(x_prompt[c * NSEQ:(c + 1) * NSEQ].reshape(NSEQ * T, D)),
            xs=np.ascontiguousarray(x_sample[c]),
            ckT=ckT,
            cv=np.ascontiguousarray(cvv[c].reshape(P, 512)),
            sconvT=np.ascontiguousarray(sc[c].reshape(30, 4, 128).transpose(2, 1, 0)),
            sffnT=np.ascontiguousarray(sf[c].reshape(2, NFC, 128).transpose(2, 1, 0)),
            w_in=wi, w_out=wo, w_up=wu, w_down=wd, consts=consts, smallp=sm))
    res = run_bass_kernel_spmd(nc, in_maps, core_ids=list(range(n_cores)))
    R = res.results
    y_p = np.concatenate([r["yp"].reshape(NSEQ, T, D) for r in R], 0)
    y_s = np.stack([r["ys"] for r in R], 0)
    k_p = np.concatenate([r["kp"].reshape(NSEQ, T, 8, 64) for r in R], 0)[None]
    v_p = np.concatenate([r["vp"].reshape(NSEQ, T, 8, 64) for r in R], 0)[None]
    k_s = np.stack([r["ks"].reshape(32, 8, 64) for r in R], 0)[None]
    v_s = np.stack([r["vs"].reshape(32, 8, 64) for r in R], 0)[None]
    c_p = np.concatenate([r["cstp"].transpose(0, 3, 2, 1).reshape(NSEQ, 30, 512) for r in R], 0)[None]
    c_s = np.stack([r["csts"].transpose(2, 1, 0).reshape(30, 512) for r in R], 0)[None]
    f_p = np.concatenate([r["fstp"].transpose(0, 3, 2, 1).reshape(NSEQ, 2, 2 * DFF) for r in R], 0)[None]
    f_s = np.stack([r["fsts"].transpose(2, 1, 0).reshape(2, 2 * DFF) for r in R], 0)[None]
    outs = (y_p, y_s, k_p, v_p, k_s, v_s, c_p, c_s, f_p, f_s)
    return tuple(np.ascontiguousarray(o, dtype=np.float32) for o in outs)


def kernel(**inputs):
    return run_cores(**inputs)
```

```python
import contextlib
import numpy as np
import concourse.bass as bass
import concourse.mybir as mybir
from concourse.bass_utils import run_bass_kernel_spmd

F32 = mybir.dt.float32
BF16 = mybir.dt.bfloat16
F32R = mybir.dt.float32r
AF = mybir.ActivationFunctionType
ALU = mybir.AluOpType
AX = mybir.AxisListType

D = 1024
IN = 2560
DFF = 2816
NFC = 44
EPS = 1e-6
SB_BASE = 16512
SB_END = 229344
ENGS = ['pe', 'act', 'dve', 'pool', 'sp']


class Prog:
    def __init__(self):
        self.ins = []
        self.lw = {}
        self.rd = {}
        self.last_eng = {}
        self.pending_dma = []
        self.bar = None
        self.dma_keys = {}

    def add(self, eng, fn, r=(), w=(), dma=None):
        al = getattr(self, 'alias', None)
        if al:
            r = list(r) + [p for k in r for p in al.get(k, ())]
            w = list(w) + [p for k in w for p in al.get(k, ())]
        i = len(self.ins)
        deps = set()
        if self.bar is not None:
            deps.add(self.bar)
        for k in r:
            x = self.lw.get(k)
            if x is not None:
                deps.add(x)
        for k in w:
            x = self.lw.get(k)
            if x is not None:
                deps.add(x)
            rr = self.rd.get(k)
            if rr:
                deps.update(rr.values())
        rk = ('dma', i) if dma is not None else eng
        for k in r:
            self.rd.setdefault(k, {})[rk] = i
        for k in w:
            self.lw[k] = i
            self.rd[k] = {}
        if dma is not None:
            if dma not in self.dma_keys:
                self.dma_keys[dma] = len(self.dma_keys)
            self.pending_dma.append(i)
        self.last_eng[eng] = i
        self.ins.append(dict(eng=eng, fn=fn, deps=deps, dma=dma))
        return i

    def barrier(self):
        deps = set(self.last_eng.values()) | set(self.pending_dma)
        i = len(self.ins)
        self.ins.append(dict(eng='sp', fn=lambda e: e.nop(), deps=deps, dma=None))
        self.last_eng = {'sp': i}
        self.pending_dma = []
        self.bar = i
        self.lw = {}
        self.rd = {}

    def emit(self, nc, stack):
        n = len(self.ins)
        sig = [False] * n
        for ins in self.ins:
            for d in ins['deps']:
                di = self.ins[d]
                if di['dma'] is not None:
                    continue
                if di['eng'] == 'pe' and ins['eng'] == 'pe':
                    continue
                sig[d] = True
        esem = {e: stack.enter_context(nc.semaphore("s_" + e)) for e in ENGS}
        dsem = {k: stack.enter_context(nc.semaphore("d_%d" % j)) for k, j in self.dma_keys.items()}
        cnt = {e: 0 for e in ENGS}
        dcnt = {k: 0 for k in self.dma_keys}
        sv = [None] * n
        for i, ins in enumerate(self.ins):
            if ins['dma'] is not None:
                dcnt[ins['dma']] += 16
                sv[i] = (dsem[ins['dma']], dcnt[ins['dma']], ('d', ins['dma']))
            elif sig[i]:
                cnt[ins['eng']] += 1
                sv[i] = (esem[ins['eng']], cnt[ins['eng']], ('e', ins['eng']))
        per = {e: [] for e in ENGS}
        for i, ins in enumerate(self.ins):
            per[ins['eng']].append(i)
        insl = self.ins

        def run(engname, eng):
            seen = {}
            for i in per[engname]:
                ins = insl[i]
                need = {}
                for d in ins['deps']:
                    s = sv[d]
                    if s is None:
                        continue
                    if need.get(s[2], (None, 0))[1] < s[1]:
                        need[s[2]] = (s[0], s[1])
                for key, (sem, val) in need.items():
                    if seen.get(key, 0) < val:
                        eng.wait_ge(sem, val)
                        seen[key] = val
                bi = ins['fn'](eng)
                if ins['dma'] is not None:
                    bi.then_inc(dsem[ins['dma']], 16)
                elif sig[i]:
                    bi.then_inc(esem[engname], 1)

        with nc.Block() as block:
            @block.tensor
            def _(e):
                run('pe', e)

            @block.scalar
            def _(e):
                run('act', e)

            @block.vector
            def _(e):
                run('dve', e)

            @block.gpsimd
            def _(e):
                run('pool', e)

            @block.sync
            def _(e):
                run('sp', e)


def build_nc(NSEQ, T, P):
    nc = bass.Bass("TRN2", target_bir_lowering=False)
    NPB = P // 128
    NTOK = NSEQ * T
    TS = 32

    def din(name, shape):
        return nc.dram_tensor(name, list(shape), F32, kind="ExternalInput").ap()

    def dout(name, shape):
        return nc.dram_tensor(name, list(shape), F32, kind="ExternalOutput").ap()

    xp = din("xp", [NTOK, D])
    xs = din("xs", [TS, D])
    ckT = din("ckT", [128, 4, P])
    cv = din("cv", [P, 512])
    sconvT = din("sconvT", [128, 4, 30])
    sffnT = din("sffnT", [128, NFC, 2])
    w_in = din("w_in", [D, IN])
    w_out = din("w_out", [D, D])
    w_up = din("w_up", [D, 2 * DFF])
    w_down = din("w_down", [DFF, D])
    consts = din("consts", [128, 4, 128])
    sm = din("smallp", [128, 512])
    yp = dout("yp", [NTOK, D])
    ys = dout("ys", [TS, D])
    kp = dout("kp", [NTOK, 512])
    vp = dout("vp", [NTOK, 512])
    ks = dout("ks", [TS, 512])
    vs = dout("vs", [TS, 512])
    cstp = dout("cstp", [NSEQ, 128, 4, 30])
    csts = dout("csts", [128, 4, 30])
    fstp = dout("fstp", [NSEQ, 128, NFC, 2])
    fsts = dout("fsts", [128, NFC, 2])
    hscr = nc.dram_tensor("hscr", [NTOK + TS, D], F32, kind="Internal").ap()

    off = [SB_BASE]
    sb_map = {}

    def sb(name, shape, dt, at=None):
        if at is None:
            at = off[0]
        nbytes = int(np.prod(shape[1:])) * (4 if dt == F32 else 2)
        nbytes = (nbytes + 31) // 32 * 32
        t = nc.alloc_sbuf_tensor_at(name, list(shape), dt, offset=at)
        assert at + nbytes <= SB_END, (name, at, nbytes)
        sb_map[name] = (at, nbytes)
        if at == off[0]:
            off[0] += nbytes
        return t

    c_f32 = sb("c_f32", [128, 4, 128], F32)
    smp = sb("smp", [128, 512], F32)
    Lb = sb("Lb", [128, 128], BF16)
    onesb = sb("onesb", [128, 128], BF16)
    zerosb = sb("zerosb", [128, 64], BF16)
    identb = sb("identb", [128, 128], BF16)
    negmb = sb("negmb", [128, 128], BF16)
    onesnb = sb("onesnb", [128, 128], BF16)
    mask2 = sb("mask2", [128, 2, 128], F32)
    stat = sb("stat", [128, 64], F32)
    halo = sb("halo", [128, NFC, 2], F32)
    ident = c_f32[:, 0, :]
    onesf = c_f32[:, 2, :]
    gmix = smp[:, 0:8]
    gffn = smp[:, 8:16]
    gq = smp[:, 16:80]
    gk = smp[:, 80:144]
    wdw = smp[:, 144:268].rearrange("p (c t) -> p c t", t=31)
    bdw = smp[:, 268:272]
    gln = smp[:, 272:276]
    bln = smp[:, 276:280]
    wfd = smp[:, 280:412].rearrange("p (c t) -> p c t", t=3)
    bfd = smp[:, 412:456]
    PH = off[0]

    off[0] = PH
    w_in_sb = sb("w_in_sb", [128, 8, IN], BF16)
    w_out_sb = sb("w_out_sb", [128, 8, D], BF16)
    KTC = max(T, P + 128)
    NVB = max(T // 128, NPB + 1)
    kT = sb("kT", [128, 4, KTC], BF16)
    v_sb = sb("v_sb", [128, NVB, 512], BF16)
    qT = sb("qT", [128, 4, 512], BF16)
    ccT = sb("ccT", [128, 4, 512], BF16)
    attnT = sb("attnT", [128, 4, 512], BF16)
    cbuf = sb("cbuf", [128, 4, 542], F32)
    xnT = sb("xnT", [128, 8, 512], BF16)
    cvb = sb("cvb", [128, 4, 512], F32)
    OV = off[0]
    sq2 = [sb("sq%d" % i, [128, 512], F32) for i in range(2)]
    tq2 = [sb("tq%d" % i, [128, 512], F32) for i in range(2)]
    kst = [sb("kst%d" % i, [128, 512], F32) for i in range(2)]
    vst = [sb("vst%d" % i, [128, 512], F32) for i in range(2)]
    sg = [sb("sg%d" % i, [128, 512], F32) for i in range(2)]
    lnm = sb("lnm", [128, 512], F32)
    lnr = sb("lnr", [128, 512], F32)
    lnt_off = off[0]
    lnt = [sb("lnt%d" % i, [128, 512], F32) for i in range(2)]
    assert off[0] <= OV + 30720
    off[0] = OV + 30720
    xring = [sb("xring0", [128, D], F32)]
    xnb4A = sb("xnb4A", [128, 4, D], BF16)
    endA1 = off[0]
    off[0] = OV
    e_b = [sb("e%d" % i, [128, 2, 512], F32) for i in range(3)]
    sp_b = [sb("sp%d" % i, [128, 2, 512], BF16) for i in range(2)]
    R_b = sb("R", [128, 2, 512], BF16)
    ec_b = [sb("ec%d" % i, [128, 2, 512], F32) for i in range(2)]
    a_b = [sb("a%d" % i, [128, 2, 512], BF16) for i in range(2)]
    endA2 = off[0]
    assert endA2 <= OV + 30720
    off[0] = OV
    wst = [sb("wst%d" % i, [128, 1280], F32) for i in range(4)]
    off[0] = PH
    w_up_sb = sb("w_up_sb", [128, 8, 2 * DFF], BF16)
    w_down_sb = sb("w_down_sb", [128, 22, D], BF16)
    hring = [sb("hring%d" % i, [128, D], F32) for i in range(2)]
    xnb4B = sb("xnb4B", [128, 4, D], BF16)
    hnT = sb("hnT", [128, 8, 512], BF16)
    actT = sb("actT", [128, 22, 512], BF16)
    wstB = [sb("wstB%d" % i, [128, 1280], F32, at=off[0] - 22 * 1024 + i * 5120) for i in range(4)]
    upb = [sb("upb%d" % i, [128, 520], F32) for i in range(3)]
    tba = [sb("tba%d" % i, [128, 512], F32) for i in range(1)]
    tbg = [sb("tbg%d" % i, [128, 512], F32) for i in range(3)]
    yst = [sb("yst%d" % i, [128, D], F32) for i in range(2)]
    assert max(endA1, endA2, off[0]) <= SB_END

    PS = [nc.alloc_psum_tensor("ps%d" % i, [128, 2, 512], F32) for i in range(4)]

    def bank(i):
        return PS[i // 2][:, i % 2, :]

    P_ = Prog()
    add = P_.add
    alias = {}
    for nm in (['sq0', 'sq1', 'tq0', 'tq1', 'kst0', 'kst1', 'vst0', 'vst1', 'sg0', 'sg1', 'lnm', 'lnr', 'lnt0', 'lnt1'] +
               ['e0', 'e1', 'e2', 'sp0', 'sp1', 'R', 'ec0', 'ec1', 'a0', 'a1']):
        at_, nb_ = sb_map[nm]
        assert OV <= at_ and at_ + nb_ <= OV + 30720, nm
        alias[nm] = ['pg%d' % p for p in range((at_ - OV) // 2048, (at_ + nb_ - 1 - OV) // 2048 + 1)]
    P_.alias = alias

    def mm(out, lhsT, rhs, start, stop, r, w):
        add('pe', lambda e: e.matmul(out, lhsT, rhs, start=start, stop=stop), r=r, w=w)

    def tr(out, in_, idn, r, w):
        add('pe', lambda e: e.transpose(out, in_, idn), r=r, w=w)

    def act(out, in_, func, r, w, bias=0.0, scale=1.0):
        add('act', lambda e: e.activation(out, in_, func, bias=bias, scale=scale), r=r, w=w)

    def tt(eng, out, in0, in1, op, r, w):
        add(eng, lambda e: e.tensor_tensor(out, in0, in1, op), r=r, w=w)

    def ts(eng, out, in0, s1, s2, op0, op1, r, w):
        add(eng, lambda e: e.tensor_scalar(out, in0, s1, s2, op0, op1), r=r, w=w)

    def stt(out, in0, scalar, in1, op0, op1, r, w):
        add('dve', lambda e: e.scalar_tensor_tensor(out, in0, scalar, in1, op0, op1), r=r, w=w)

    def cp(eng, out, in_, r, w):
        if eng == 'act':
            add('act', lambda e: e.activation(out, in_, AF.Copy), r=r, w=w)
        else:
            add(eng, lambda e: e.tensor_copy(out, in_), r=r, w=w)

    def mset(eng, ap, val, w):
        add(eng, lambda e: e.memset(ap, val), w=w)

    def rsum(out, in_, r, w):
        add('dve', lambda e: e.reduce_sum(out, in_, AX.X), r=r, w=w)

    def dma(out, in_, key, r=(), w=()):
        add('sp', lambda e: e.dma_start(out, in_), r=r, w=w, dma=key)

    def rstd_from(ssap, outap, inv_n, key):
        act(outap, ssap, AF.Ln, r=[key], w=[key], bias=EPS, scale=inv_n)
        act(outap, outap, AF.Exp, r=[key], w=[key], scale=-0.5)

    rot = [0]

    def nbank(lo, hi):
        b = lo + rot[0] % (hi - lo)
        rot[0] += 1
        return b

    dma(c_f32[:], consts, 'setup', w=['c_f32'])
    dma(smp[:], sm, 'setup2', w=['smp'])
    ts('dve', Lb[:], c_f32[:, 1, :], -1.0, None, ALU.mult, ALU.bypass, r=['c_f32'], w=['Lb'])
    ts('dve', onesnb[:], c_f32[:, 2, :], -1.0, None, ALU.mult, ALU.bypass, r=['c_f32'], w=['onesnb'])
    ts('dve', smp[:, 16:80], smp[:, 16:80], 0.125, None, ALU.mult, ALU.bypass, r=['smp'], w=['smp'])
    cp('dve', onesb[:], c_f32[:, 2, :], r=['c_f32'], w=['onesb'])
    mset('pool', zerosb[:], 0.0, w=['zerosb'])
    cp('dve', identb[:], c_f32[:, 0, :], r=['c_f32'], w=['identb'])
    ts('dve', negmb[:], c_f32[:, 3, :], -1.0, 30000.0, ALU.add, ALU.mult, r=['c_f32'], w=['negmb'])
    mset('pool', stat[:, 63:64], -0.5, w=['mhalf'])
    cp('dve', mask2[:, 0, :], c_f32[:, 3, :], r=['c_f32'], w=['mask2a'])
    cp('dve', mask2[:, 1, :], c_f32[:, 3, :], r=['c_f32'], w=['mask2b'])
    P_.barrier()

    def load_weight(dst, src, nk, ncols, scale, stg, tag):
        cnt = 0
        for kc in range(nk):
            for c0 in range(0, ncols, 1280):
                cw = min(1280, ncols - c0)
                j = cnt % len(stg)
                cnt += 1
                s = stg[j]
                dma(s[:, 0:cw], src[kc * 128:(kc + 1) * 128, c0:c0 + cw], tag + str(j), w=[tag + str(j)])
                if cnt % 3 != 0:
                    if scale is not None:
                        ts('dve', dst[:, kc, c0:c0 + cw], s[:, 0:cw], scale[:, kc:kc + 1], None, ALU.mult, ALU.bypass,
                           r=[tag + str(j)], w=[tag + 'dst'])
                    else:
                        cp('dve', dst[:, kc, c0:c0 + cw], s[:, 0:cw], r=[tag + str(j)], w=[tag + 'dst'])
                else:
                    if scale is not None:
                        act(dst[:, kc, c0:c0 + cw], s[:, 0:cw], AF.Identity, r=[tag + str(j)], w=[tag + 'dst'],
                            scale=scale[:, kc:kc + 1])
                    else:
                        cp('act', dst[:, kc, c0:c0 + cw], s[:, 0:cw], r=[tag + str(j)], w=[tag + 'dst'])

    load_weight(w_in_sb, w_in, 8, IN, gmix, wst, 'wst')
    load_weight(w_out_sb, w_out, 8, D, None, wst, 'wst')
    P_.barrier()

    def norm_dma(src, row0, s, tp, ring, pfx):
        j = s % len(ring)
        rk = pfx + 'ring%d' % j
        dma(ring[j][0:tp, :], src[row0 + s * tp: row0 + (s + 1) * tp, :], rk, w=[rk])

    def norm_cmp(s, tp, ring, xnb4, pfx):
        j = s % len(ring)
        rb = ring[j]
        rk = pfx + 'ring%d' % j
        xk = pfx + 'xn%d' % s
        xb = xnb4[0:tp, s, :]
        act(xb, rb[0:tp, :], AF.Square, r=[rk], w=[xk])
        ssap = stat[0:tp, s:s + 1]
        skey = pfx + 'ss%d' % s
        rsum(ssap, xb, r=[xk], w=[skey])
        ts('pool', ssap, ssap, 1.0 / D, EPS, ALU.mult, ALU.add, r=[skey], w=[skey])
        tt('pool', ssap, ssap, stat[0:tp, 63:64], ALU.pow, r=[skey, 'mhalf'], w=[skey])
        ts('dve', xb, rb[0:tp, :], ssap, None, ALU.mult, ALU.bypass, r=[rk, skey, xk], w=[xk])

    def norm_chain(src, row0, s, tp, ring, xnb4, pfx):
        norm_dma(src, row0, s, tp, ring, pfx)
        norm_cmp(s, tp, ring, xnb4, pfx)

    def norm_tr(s, tp, xnb4, outT, pfx):
        xk = pfx + 'xn%d' % s
        for b in range(2):
            bkb = bank(b).bitcast(BF16)
            for q in range(4):
                kc = 4 * b + q
                tr(bkb[:, q * 128: q * 128 + tp], xnb4[0:tp, s, kc * 128:(kc + 1) * 128], identb[0:tp, 0:tp],
                   r=[xk, 'identb'], w=['bank%d' % b])
            src_ap = bkb[:, 0:512].rearrange("p (a b) -> p a b", b=128)[:, :, 0:tp]
            cp('act' if (b == 0 or pfx == 'A') else 'dve', outT[:, 4 * b:4 * b + 4, s * tp:(s + 1) * tp], src_ap,
               r=['bank%d' % b], w=[pfx + 'outT'])

    hn_cnt = [0]

    def head_norm(bk, bkey, tp, gvec, dst, dkey):
        jj = hn_cnt[0] % 2
        hn_cnt[0] += 1
        sq = sq2[jj]
        tq = tq2[jj]
        sqk = 'sq%d' % jj
        tqk = 'tq%d' % jj
        s8k = 'ss8%d' % jj
        act(sq[0:tp, :], bk[0:tp, :], AF.Square, r=[bkey], w=[sqk])
        ss8 = stat[0:tp, 8 + 8 * jj:16 + 8 * jj]
        rsum(ss8, sq[0:tp, :].rearrange("p (h d) -> p h d", d=64), r=[sqk], w=[s8k])
        rstd_from(ss8, ss8, 1.0 / 64, s8k)
        tt('dve', tq[0:tp, :].rearrange("p (h d) -> p h d", d=64), bk[0:tp, :].rearrange("p (h d) -> p h d", d=64),
           ss8.unsqueeze(2).to_broadcast([tp, 8, 64]), ALU.mult, r=[bkey, s8k], w=[tqk])
        tt('pool', dst[0:tp, :].rearrange("p (h d) -> p h d", d=64), tq[0:tp, :].rearrange("p (h d) -> p h d", d=64),
           gvec[0:tp, :].unsqueeze(1).to_broadcast([tp, 8, 64]), ALU.mult, r=[tqk, 'smp'], w=[dkey])

    def inproj(x_src, row0, ntok, nsub, tp, kcol0, vblk0, kout, vout, first_tile, conv_in_state, inter):
        for s_ in range(nsub):
            norm_tr(s_, tp, xnb4A, xnT, 'A')
        stc = [0]

        def group(s, g):
            b = nbank(2, 8)
            bk = bank(b)
            bkey = 'bank%d' % b
            for kc in range(8):
                mm(bk[0:tp, :], xnT[:, kc, s * tp:(s + 1) * tp], w_in_sb[:, kc, g * 512:(g + 1) * 512],
                   kc == 0, kc == 7, r=['AoutT', 'w_in'], w=[bkey])
            j = stc[0] % 2
            stc[0] += 1
            if g < 2:
                st_ = kst[j]
                skey = 'kst%d' % j
                head_norm(bk, bkey, tp, gq if g == 0 else gk, st_, skey)
                if g == 1:
                    dma(kout[row0 + s * tp: row0 + (s + 1) * tp, :], st_[0:tp, :], skey, r=[skey])

                def post():
                    b2 = nbank(2, 8)
                    bk2 = bank(b2)
                    for pr in range(4):
                        tr(bk2[:, pr * 128: pr * 128 + tp], st_[0:tp, pr * 128:(pr + 1) * 128], ident[0:tp, 0:tp],
                           r=[skey, 'c_f32'], w=['bank%d' % b2])
                    src_ap = bk2.rearrange("p (a b) -> p a b", b=128)[:, :, 0:tp]
                    if g == 0:
                        cp('act', qT[:, :, s * tp:(s + 1) * tp], src_ap, r=['bank%d' % b2], w=['qT'])
                    else:
                        c0 = kcol0 + s * tp
                        cp('act', kT[:, :, c0:c0 + tp], src_ap, r=['bank%d' % b2], w=['kT'])
                return post
            else:
                st_ = vst[j]
                skey = 'vst%d' % j
                cp('act', st_[0:tp, :], bk[0:tp, :], r=[bkey], w=[skey])
                dma(vout[row0 + s * tp: row0 + (s + 1) * tp, :], st_[0:tp, :], skey, r=[skey])
                cp('pool', v_sb[0:tp, vblk0 + s, :], st_[0:tp, :], r=[skey], w=['v_sb'])
                return None

        def fchunk(c):
            bu = nbank(2, 8)
            bg = nbank(2, 8)
            for kc in range(8):
                mm(bank(bg)[:, 0:ntok], w_in_sb[:, kc, 2048 + c * 128: 2048 + (c + 1) * 128], xnT[:, kc, 0:ntok],
                   kc == 0, kc == 7, r=['AoutT', 'w_in'], w=['bank%d' % bg])
            j = c % 2
            act(sg[j][:, 0:ntok], bank(bg)[:, 0:ntok], AF.Sigmoid, r=['bank%d' % bg], w=['sg%d' % j])
            for kc in range(8):
                mm(bank(bu)[:, 0:ntok], w_in_sb[:, kc, 1536 + c * 128: 1536 + (c + 1) * 128], xnT[:, kc, 0:ntok],
                   kc == 0, kc == 7, r=['AoutT', 'w_in'], w=['bank%d' % bu])
            tt('dve', cbuf[:, c, 30:30 + ntok], bank(bu)[:, 0:ntok], sg[j][:, 0:ntok], ALU.mult,
               r=['bank%d' % bu, 'sg%d' % j], w=['cbuf_c%d' % c])

        groups = [(s, g) for s in range(nsub) for g in range(3)]
        pend = None
        ii = 0
        for idx, (s, g) in enumerate(groups):
            post = group(s, g)
            for _ in range(3):
                if ii < len(inter):
                    inter[ii]()
                    ii += 1
            if pend is not None:
                pend()
            pend = post
        if pend is not None:
            pend()
        while ii < len(inter):
            inter[ii]()
            ii += 1

        def part2():
            if first_tile:
                if conv_in_state is None:
                    mset('pool', cbuf[:, :, 0:30], 0.0, w=['cbuf_h'])
                else:
                    dma(cbuf[:, :, 0:30], conv_in_state, 'cbufin', w=['cbuf_h'])
            for c in range(4):
                fchunk(c)
        return part2

    def conv_ops(ntok):
        ops = []

        def mk(tau, c):
            def f():
                if tau == 0:
                    ts('dve', cvb[:, c, 0:ntok], cbuf[:, c, 0:ntok], wdw[:, c, 0:1], bdw[:, c:c + 1], ALU.mult, ALU.add,
                       r=['cbuf_c%d' % c, 'cbuf_h', 'smp'], w=['cv%d' % c])
                else:
                    stt(cvb[:, c, 0:ntok], cbuf[:, c, tau:tau + ntok], wdw[:, c, tau:tau + 1], cvb[:, c, 0:ntok],
                        ALU.mult, ALU.add, r=['cbuf_c%d' % c, 'cbuf_h', 'cv%d' % c], w=['cv%d' % c])
            return f
        for tau in range(31):
            for c in range(4):
                ops.append(mk(tau, c))
        return ops

    def conv_ln(ntok):
        b1 = nbank(2, 8)
        b2 = nbank(2, 8)
        for c in range(4):
            j = c % 2
            v16 = sg[j][:].bitcast(BF16)
            cp('act', v16[:, 0:ntok], cvb[:, c, 0:ntok], r=['cv%d' % c], w=['sg%d' % j])
            act(v16[:, 512:512 + ntok], cvb[:, c, 0:ntok], AF.Square, r=['cv%d' % c], w=['sg%d' % j])
            mm(bank(b1)[:, 0:ntok], onesb[:], v16[:, 0:ntok], c == 0, c == 3, r=['sg%d' % j, 'onesb'], w=['bank%d' % b1])
            mm(bank(b2)[:, 0:ntok], onesb[:], v16[:, 512:512 + ntok], c == 0, c == 3, r=['sg%d' % j, 'onesb'], w=['bank%d' % b2])
        ts('dve', lnm[:, 0:ntok], bank(b1)[:, 0:ntok], 1.0 / 512, None, ALU.mult, ALU.bypass, r=['bank%d' % b1], w=['lnm'])
        tt('dve', lnt[0][:, 0:ntok], lnm[:, 0:ntok], lnm[:, 0:ntok], ALU.mult, r=['lnm'], w=['lnt0'])
        stt(lnr[:, 0:ntok], bank(b2)[:, 0:ntok], 1.0 / 512, lnt[0][:, 0:ntok], ALU.mult, ALU.subtract,
            r=['bank%d' % b2, 'lnt0'], w=['lnr'])
        ts('dve', lnr[:, 0:ntok], lnr[:, 0:ntok], 0.0, None, ALU.max, ALU.bypass, r=['lnr'], w=['lnr'])
        act(lnr[:, 0:ntok], lnr[:, 0:ntok], AF.Ln, r=['lnr'], w=['lnr'], bias=EPS)
        act(lnr[:, 0:ntok], lnr[:, 0:ntok], AF.Exp, r=['lnr'], w=['lnr'], scale=-0.5)
        for c in range(4):
            j = c % 2
            tt('dve', lnt[j][:, 0:ntok], cvb[:, c, 0:ntok], lnm[:, 0:ntok], ALU.subtract, r=['cv%d' % c, 'lnm'], w=['lnt%d' % j])
            tt('dve', lnt[j][:, 0:ntok], lnt[j][:, 0:ntok], lnr[:, 0:ntok], ALU.mult, r=['lnt%d' % j, 'lnr'], w=['lnt%d' % j])
            act(ccT[:, c, 0:ntok], lnt[j][:, 0:ntok], AF.Silu, r=['lnt%d' % j, 'smp'], w=['ccT'],
                bias=bln[:, c:c + 1], scale=gln[:, c:c + 1])

    def attention(N, blocks, extra, tail=(), head=()):
        steps = []
        for pr in range(4):
            for bi, (kb, c0, dg) in enumerate(blocks):
                steps.append((pr, kb, c0, dg, bi == 0, bi == len(blocks) - 1))
        n = len(steps)

        def stageA(m):
            pr, kb, c0, dg, first, last = steps[m]
            Z = PS[0]
            zk = 'Zb'
            e = e_b[m % 3]
            ek = 'e%d' % (m % 3)
            for h in range(2):
                mm(Z[:, h, c0:N], kT[64 * h:64 * h + 64, pr, kb * 128:(kb + 1) * 128], qT[64 * h:64 * h + 64, pr, c0:N],
                   True, not dg, r=['kT', 'qT'], w=['bank0', 'bank1'])
            if dg:
                mw = min(128, N - c0)
                for h in range(2):
                    mm(Z[:, h, c0:c0 + mw], identb[:], negmb[:, 0:mw], False, True, r=['identb', 'negmb'], w=['bank0', 'bank1'])
            act(Z[:, :, c0:N], Z[:, :, c0:N], AF.Exp, r=['bank0', 'bank1'], w=['bank0', 'bank1'])
            act(sp_b[m % 2][:, :, c0:N], Z[:, :, c0:N], AF.Ln, r=['bank0', 'bank1'], w=['sp%d' % (m % 2)], bias=1.0)

        def stageB(m):
            pr, kb, c0, dg, first, last = steps[m]
            C = PS[1 + m % 2]
            ckl = ['bank2', 'bank3'] if m % 2 == 0 else ['bank4', 'bank5']
            spk = 'sp%d' % (m % 2)
            spt = sp_b[m % 2]
            if first:
                mset('pool', R_b[:, :, 0:N], 0.0, w=['R'])
            for h in range(2):
                mm(C[:, h, c0:N], Lb[:], spt[:, h, c0:N], True, False, r=[spk, 'Lb'], w=ckl)
                if not first:
                    mm(C[:, h, c0:N], onesnb[:], R_b[:, h, c0:N], False, False, r=['R', 'onesnb'], w=ckl)
            for h in range(2):
                mm(C[:, h, c0:N], kT[64 * h:64 * h + 64, pr, kb * 128:(kb + 1) * 128], qT[64 * h:64 * h + 64, pr, c0:N],
                   False, not dg, r=['kT', 'qT'], w=ckl)
            if dg:
                mw = min(128, N - c0)
                for h in range(2):
                    mm(C[:, h, c0:c0 + mw], identb[:], negmb[:, 0:mw], False, True, r=['identb', 'negmb'], w=ckl)
            if not last:
                tt('dve', R_b[:, :, c0:N], R_b[:, :, c0:N], spt[:, :, c0:N], ALU.add, r=[spk, 'R'], w=['R'])
            act(a_b[m % 2][:, :, c0:N], C[:, :, c0:N], AF.Exp, r=ckl, w=['a%d' % (m % 2)])

        def stageC(m):
            pr, kb, c0, dg, first, last = steps[m]
            O = PS[3][:, 0, :]
            ok = 'bank6'
            if first:
                for h in range(2):
                    mm(O[64 * h:64 * h + 64, 0:N], zerosb[:, 0:64], w_out_sb[:, 0, 0:N], True, False,
                       r=['zerosb', 'w_out'], w=[ok])
            for h in range(2):
                mm(O[64 * h:64 * h + 64, c0:N], v_sb[:, kb, (2 * pr + h) * 64:(2 * pr + h + 1) * 64], a_b[m % 2][:, h, c0:N],
                   False, last, r=['v_sb', 'a%d' % (m % 2)], w=[ok])
            if last:
                cp('dve', attnT[:, pr, 0:N], O[:, 0:N], r=[ok], w=['attnT'])

        extra = list(extra)
        xi = 0
        hstride = max(1, min(3, (len(blocks) - 1) // max(1, len(head)))) if head else 1
        per_step = max(1, min(4, -(-len(extra) // (n + 1))))
        for it in range(n + 2):
            if it < n:
                stageA(it)
            if 0 <= it - 1 < n:
                stageB(it - 1)
            if head and it % hstride == 0 and it // hstride < len(head):
                head[it // hstride]()
            for _ in range(per_step):
                if xi < len(extra):
                    extra[xi]()
                    xi += 1
            if 0 <= it - 2 < n:
                stageC(it - 2)
            t0_ = max(0, n - 3 * len(tail) - 2)
            if it >= t0_ and (it - t0_) % 3 == 0 and (it - t0_) // 3 < len(tail):
                tail[(it - t0_) // 3]()
        for i_ in range(len(tail)):
            if n + 2 <= t0_ + 3 * i_:
                tail[i_]()
        return extra[xi:]

    def outproj(x_src, row0, hrow0, nsub, tp, deferred=False):
        stg = [(kst[0], 'kst0'), (kst[1], 'kst1'), (vst[0], 'vst0'), (vst[1], 'vst1')]
        items = [(s, half) for s in range(nsub) for half in range(2)]
        if deferred:
            def mk(i, s, half):
                def f():
                    sbuf_, hk = stg[i % 2]
                    cs = slice(half * 512, (half + 1) * 512)
                    dma(sbuf_[0:tp, :], x_src[row0 + s * tp: row0 + (s + 1) * tp, cs], hk, w=[hk])
                    bk = bank(7)
                    for kc in range(8):
                        src = attnT if kc < 4 else ccT
                        mm(bk[0:tp, :], src[:, kc % 4, s * tp:(s + 1) * tp], w_out_sb[:, kc, cs],
                           kc == 0, kc == 7, r=['attnT', 'ccT', 'w_out'], w=['bank7'])
                    tt('dve', sbuf_[0:tp, :], bk[0:tp, :], sbuf_[0:tp, :], ALU.add, r=['bank7', hk], w=[hk])
                    dma(hscr[hrow0 + s * tp: hrow0 + (s + 1) * tp, cs], sbuf_[0:tp, :], hk, r=[hk])
                return f
            return [mk(i, s, half) for i, (s, half) in enumerate(items)]

        def ld(i):
            s, half = items[i]
            sbuf_, hk = stg[i % 4]
            dma(sbuf_[0:tp, :], x_src[row0 + s * tp: row0 + (s + 1) * tp, half * 512:(half + 1) * 512], hk, w=[hk])
        for i in range(min(4, len(items))):
            ld(i)
        for i, (s, half) in enumerate(items):
            if True:
                sbuf_, hk = stg[i % 4]
                cs = slice(half * 512, (half + 1) * 512)
                b = nbank(0, 8)
                bk = bank(b)
                for kc in range(8):
                    src = attnT if kc < 4 else ccT
                    mm(bk[0:tp, :], src[:, kc % 4, s * tp:(s + 1) * tp], w_out_sb[:, kc, cs],
                       kc == 0, kc == 7, r=['attnT', 'ccT', 'w_out'], w=['bank%d' % b])
                tt('dve', sbuf_[0:tp, :], bk[0:tp, :], sbuf_[0:tp, :], ALU.add, r=['bank%d' % b, hk], w=[hk])
                dma(hscr[hrow0 + s * tp: hrow0 + (s + 1) * tp, cs], sbuf_[0:tp, :], hk, r=[hk])
                if i + 4 < len(items):
                    ld(i + 4)

    seqs = []
    for sidx in range(NSEQ):
        seqs.append(dict(x=xp, row0=sidx * T, hrow0=sidx * T, T=T, tile=512, tp=128, kout=kp, vout=vp,
                         cst=cstp[sidx], fst=fstp[sidx], yout=yp, past=0, conv_state=None, ffn_state=None))
    seqs.append(dict(x=xs, row0=0, hrow0=NTOK, T=TS, tile=TS, tp=TS, kout=ks, vout=vs,
                     cst=csts, fst=fsts, yout=ys, past=NPB, conv_state=sconvT, ffn_state=sffnT))

    tilesA = []
    for sq_ in seqs:
        for ti in range(sq_['T'] // sq_['tile']):
            tilesA.append((sq_, ti))

    def chainA(k):
        sq_, ti = tilesA[k]
        for s_ in range(sq_['tile'] // sq_['tp']):
            norm_chain(sq_['x'], sq_['row0'] + ti * sq_['tile'], s_, sq_['tp'], xring, xnb4A, 'A')

    def seq_start(sq_):
        past = sq_['past']
        if past:
            for blk in range(past):
                j = blk % 2
                dma(kst[j][:].rearrange("p (a b) -> p a b", b=128), ckT[:, :, blk * 128:(blk + 1) * 128], 'kst%d' % j, w=['kst%d' % j])
                cp('dve', kT[:, :, blk * 128:(blk + 1) * 128], kst[j][:].rearrange("p (a b) -> p a b", b=128), r=['kst%d' % j], w=['kT'])
                dma(vst[j][:], cv[blk * 128:(blk + 1) * 128, :], 'vst%d' % j, w=['vst%d' % j])
                cp('act', v_sb[:, blk, :], vst[j][:], r=['vst%d' % j], w=['v_sb'])
            mset('pool', kT[:, :, past * 128:(past + 1) * 128], 0.0, w=['kT'])
            mset('pool', v_sb[:, past, :], 0.0, w=['v_sb'])

    def tile_params(k):
        sq_, ti = tilesA[k]
        ntok = sq_['tile']
        tp = sq_['tp']
        past = sq_['past']
        return dict(sq=sq_, ti=ti, ntile=sq_['T'] // ntok, ntok=ntok, tp=tp, nsub=ntok // tp, past=past,
                    row0=sq_['row0'] + ti * ntok, kcol0=past * 128 + ti * ntok,
                    vblk0=past + ti * (ntok // 128 if tp == 128 else 0))

    def do_inproj(k, inter):
        t = tile_params(k)
        sq_ = t['sq']
        if t['ti'] == 0:
            seq_start(sq_)
        return inproj(sq_['x'], t['row0'], t['ntok'], t['nsub'], t['tp'], t['kcol0'], t['vblk0'], sq_['kout'],
                      sq_['vout'], t['ti'] == 0, sq_['conv_state'], inter)

    chainA(0)
    p2 = do_inproj(0, [])
    p2()
    head_cur = []
    for kA in range(len(tilesA)):
        t = tile_params(kA)
        sq_, ti, ntok, tp, nsub, past = t['sq'], t['ti'], t['ntok'], t['tp'], t['nsub'], t['past']
        blocks = []
        if tp == 128:
            for j in range(3, -1, -1):
                blocks.append((past + ti * 4 + j, 128 * j, True))
            for kb in range(past + ti * 4 - 1, -1, -1):
                blocks.append((kb, 0, False))
        else:
            blocks.append((past, 0, True))
            for kb in range(past - 1, -1, -1):
                blocks.append((kb, 0, False))
        tail = []
        if kA + 1 < len(tilesA):
            nsq, nti = tilesA[kA + 1]
            nns = nsq['tile'] // nsq['tp']
            nrow0 = nsq['row0'] + nti * nsq['tile']

            def mk_tail(i_):
                def f():
                    if i_ >= 1:
                        norm_cmp(i_ - 1, nsq['tp'], xring, xnb4A, 'A')
                    if i_ < nns:
                        norm_dma(nsq['x'], nrow0, i_, nsq['tp'], xring, 'A')
                return f
            tail = [mk_tail(i_) for i_ in range(nns + 1)]
        left = attention(ntok, blocks, conv_ops(ntok), tail, head_cur)
        head_cur = []
        p2 = None
        if kA + 1 < len(tilesA):
            p2 = do_inproj(kA + 1, left)
        else:
            for f_ in left:
                f_()
        conv_ln(ntok)
        if ti == t['ntile'] - 1:
            dma(sq_['cst'], cbuf[:, :, ntok:ntok + 30], 'cbufout', r=['cbuf_c0', 'cbuf_c1', 'cbuf_c2', 'cbuf_c3', 'cbuf_h'])
        else:
            cp('pool', cbuf[:, :, 0:30], cbuf[:, :, ntok:ntok + 30], r=['cbuf_c0', 'cbuf_c1', 'cbuf_c2', 'cbuf_c3', 'cv0', 'cv1', 'cv2', 'cv3'], w=['cbuf_h'])
        if p2 is not None:
            p2()
        if kA + 1 < len(tilesA) and tilesA[kA + 1][1] != 0:
            head_cur = outproj(sq_['x'], t['row0'], sq_['hrow0'] + ti * ntok, nsub, tp, deferred=True)
        else:
            outproj(sq_['x'], t['row0'], sq_['hrow0'] + ti * ntok, nsub, tp)
    P_.barrier()

    load_weight(w_up_sb, w_up, 8, 2 * DFF, gffn, wstB, 'wstB')
    load_weight(w_down_sb, w_down, 22, D, None, wstB, 'wstB')
    P_.barrier()

    tilesB = []
    for sq_ in seqs:
        for ti in range(sq_['T'] // sq_['tile']):
            tilesB.append((sq_, ti))
    for kB, (sq_, ti) in enumerate(tilesB):
        ntile = sq_['T'] // sq_['tile']
        ntok = sq_['tile']
        tp = sq_['tp']
        nsub = ntok // tp
        hrow0 = sq_['hrow0'] + ti * ntok
        if kB + 1 < len(tilesB):
            nsq, nti = tilesB[kB + 1]
            nxt = (nsq['hrow0'] + nti * nsq['tile'], nsq['tp'], nsq['tile'] // nsq['tp'])
        else:
            nxt = None
        if kB == 0:
            for s_ in range(nsub):
                norm_chain(hscr, hrow0, s_, tp, hring, xnb4B, 'B')
            for s_ in range(nsub):
                norm_tr(s_, tp, xnb4B, hnT, 'B')
        if ti == 0:
            if sq_['ffn_state'] is None:
                mset('pool', halo[:], 0.0, w=['halo%d' % c_ for c_ in range(NFC)])
            else:
                dma(halo[:], sq_['ffn_state'], 'haloin', w=['halo%d' % c_ for c_ in range(NFC)])
        if True:
            ucnt = 0
            pendB = None

            def silu_mult(t_, tk, m):
                act(t_[:, 0:ntok], t_[:, 0:ntok], AF.Silu, r=[tk], w=[tk])
                tt('dve', actT[:, m, 0:ntok], t_[:, 0:ntok], actT[:, m, 0:ntok], ALU.mult,
                   r=[tk, 'actT%d' % m], w=['actT%d' % m])

            for m in range(22):
                newp = None
                for ch in (m, m + 22):
                    b = nbank(2, 8)
                    bk = bank(b)
                    bkey = 'bank%d' % b
                    for kc in range(8):
                        mm(bk[:, 0:ntok], w_up_sb[:, kc, ch * 128:(ch + 1) * 128], hnT[:, kc, 0:ntok], kc == 0, kc == 7,
                           r=['BoutT', 'w_up'], w=[bkey])
                    ju = ucnt % 3
                    ucnt += 1
                    ub = upb[ju]
                    uk = 'upb%d' % ju
                    if ch == m:
                        t_ = tba[0]
                        tk = 'tba0'
                    else:
                        t_ = tbg[m % 3]
                        tk = "tbg%d" % (m % 3)
                    cp('pool', ub[:, 0:2], halo[:, ch, :], r=['halo%d' % ch], w=[uk + 'h'])
                    cp('act', ub[:, 2:2 + ntok], bk[:, 0:ntok], r=[bkey], w=[uk])
                    act(t_[:, 0:ntok], bk[:, 0:ntok], AF.Identity, r=[bkey, 'smp'], w=[tk],
                        bias=bfd[:, ch:ch + 1], scale=wfd[:, ch, 2:3])
                    cp('pool', halo[:, ch, :], ub[:, ntok:ntok + 2], r=[uk, uk + 'h'], w=['halo%d' % ch])
                    stt(t_[:, 0:ntok], ub[:, 1:1 + ntok], wfd[:, ch, 1:2], t_[:, 0:ntok], ALU.mult, ALU.add,
                        r=[uk, uk + 'h', tk], w=[tk])
                    if ch == m:
                        stt(actT[:, m, 0:ntok], ub[:, 0:ntok], wfd[:, ch, 0:1], t_[:, 0:ntok], ALU.mult, ALU.add,
                            r=[uk, uk + 'h', tk], w=['actT%d' % m])
                    else:
                        stt(t_[:, 0:ntok], ub[:, 0:ntok], wfd[:, ch, 0:1], t_[:, 0:ntok], ALU.mult, ALU.add,
                            r=[uk, uk + 'h', tk], w=[tk])
                        newp = (t_, tk, m)
                if pendB is not None:
                    silu_mult(*pendB)
                pendB = newp
                if nxt is not None and m in (1, 6, 11, 16):
                    s_n = (1, 6, 11, 16).index(m)
                    if s_n < nxt[2]:
                        norm_dma(hscr, nxt[0], s_n, nxt[1], hring, 'B')
                if nxt is not None and m in (3, 8, 13, 18):
                    s_n = (3, 8, 13, 18).index(m)
                    if s_n < nxt[2]:
                        norm_cmp(s_n, nxt[1], hring, xnb4B, 'B')
            silu_mult(*pendB)
            if ti == ntile - 1:
                dma(sq_['fst'], halo[:], 'haloout', r=['halo%d' % c_ for c_ in range(NFC)])
            for s in range(nsub):
                if nxt is not None and s < nxt[2]:
                    norm_tr(s, nxt[1], xnb4B, hnT, 'B')
                j = s % 2
                yk = 'yst%d' % j
                dma(yst[j][0:tp, :], hscr[hrow0 + s * tp: hrow0 + (s + 1) * tp, :], yk, w=[yk])
                for half in range(2):
                    b = nbank(2, 8)
                    bk = bank(b)
                    for m in range(22):
                        mm(bk[0:tp, :], actT[:, m, s * tp:(s + 1) * tp], w_down_sb[:, m, half * 512:(half + 1) * 512],
                           m == 0, m == 21, r=['actT%d' % m, 'w_down'], w=['bank%d' % b])
                    tt('dve', yst[j][0:tp, half * 512:(half + 1) * 512], bk[0:tp, :], yst[j][0:tp, half * 512:(half + 1) * 512],
                       ALU.add, r=['bank%d' % b, yk], w=[yk])
                orow = sq_['row0'] + ti * ntok + s * tp
                dma(sq_['yout'][orow: orow + tp, :], yst[j][0:tp, :], yk, r=[yk])
    P_.barrier()

    stack = contextlib.ExitStack()
    with stack:
        P_.emit(nc, stack)
    return nc


def _consts():
    c = np.zeros((128, 4, 128), np.float32)
    i = np.arange(128)
    c[:, 0, :] = np.eye(128, dtype=np.float32)
    c[:, 1, :] = (i[:, None] >= i[None, :]).astype(np.float32)
    c[:, 2, :] = 1.0
    c[:, 3, :] = (i[None, :] > i[:, None]).astype(np.float32)
    return c


def _small(g_mix, g_ffn, g_q, g_k, w_dw, b_dw, g_ln, b_ln, w_fd, b_fd):
    sm = np.zeros((128, 512), np.float32)
    sm[:, 0:8] = g_mix.reshape(8, 128).T
    sm[:, 8:16] = g_ffn.reshape(8, 128).T
    sm[:, 16:80] = g_q[None, :]
    sm[:, 80:144] = g_k[None, :]
    sm[:, 144:268] = w_dw.reshape(31, 4, 128).transpose(2, 1, 0).reshape(128, 124)
    sm[:, 268:272] = b_dw.reshape(4, 128).T
    sm[:, 272:276] = g_ln.reshape(4, 128).T
    sm[:, 276:280] = b_ln.reshape(4, 128).T
    sm[:, 280:412] = w_fd.reshape(3, NFC, 128).transpose(2, 1, 0).reshape(128, 132)
    sm[:, 412:456] = b_fd.reshape(NFC, 128).T
    return sm


_NC_CACHE = {}


def run_cores(x_prompt, x_sample, cache_sb_k, cache_sb_v, state_conv, state_ffn_conv,
              g_mix, w_in, g_q, g_k, w_dw, b_dw, g_conv_ln, b_conv_ln, w_out,
              g_ffn, w_up, w_ffn_dw, b_ffn_dw, w_down, n_cores=8):
    f = lambda a: np.ascontiguousarray(np.asarray(a, dtype=np.float32))
    x_prompt = f(x_prompt); x_sample = f(x_sample)
    B, T, _ = x_prompt.shape
    NSEQ = B // n_cores
    P = cache_sb_k.shape[2]
    key = (NSEQ, T, P)
    if key not in _NC_CACHE:
        _NC_CACHE[key] = build_nc(NSEQ, T, P)
    nc = _NC_CACHE[key]
    consts = _consts()
    sm = _small(f(g_mix)[0], f(g_ffn)[0], f(g_q)[0], f(g_k)[0], f(w_dw)[0], f(b_dw)[0], f(g_conv_ln)[0],
                f(b_conv_ln)[0], f(w_ffn_dw)[0], f(b_ffn_dw)[0])
    wi = f(w_in)[0]; wo = f(w_out)[0]; wu = f(w_up)[0]; wd = f(w_down)[0]
    ck = f(cache_sb_k)[0]; cvv = f(cache_sb_v)[0]; sc = f(state_conv)[0]; sf = f(state_ffn_conv)[0]
    in_maps = []
    for c in range(n_cores):
        ckT = np.ascontiguousarray(ck[c].reshape(P, 4, 128).transpose(2, 1, 0))
        in_maps.append(dict(
            xp=np.ascontiguousarray(x_prompt[c * NSEQ:(c + 1) * NSEQ].reshape(NSEQ * T, D)),
            xs=np.ascontiguousarray(x_sample[c]),
            ckT=ckT,
            cv=np.ascontiguousarray(cvv[c].reshape(P, 512)),
            sconvT=np.ascontiguousarray(sc[c].reshape(30, 4, 128).transpose(2, 1, 0)),
            sffnT=np.ascontiguousarray(sf[c].reshape(2, NFC, 128).transpose(2, 1, 0)),
            w_in=wi, w_out=wo, w_up=wu, w_down=wd, consts=consts, smallp=sm))
    res = run_bass_kernel_spmd(nc, in_maps, core_ids=list(range(n_cores)))
    R = res.results
    y_p = np.concatenate([r["yp"].reshape(NSEQ, T, D) for r in R], 0)
    y_s = np.stack([r["ys"] for r in R], 0)
    k_p = np.concatenate([r["kp"].reshape(NSEQ, T, 8, 64) for r in R], 0)[None]
    v_p = np.concatenate([r["vp"].reshape(NSEQ, T, 8, 64) for r in R], 0)[None]
    k_s = np.stack([r["ks"].reshape(32, 8, 64) for r in R], 0)[None]
    v_s = np.stack([r["vs"].reshape(32, 8, 64) for r in R], 0)[None]
    c_p = np.concatenate([r["cstp"].transpose(0, 3, 2, 1).reshape(NSEQ, 30, 512) for r in R], 0)[None]
    c_s = np.stack([r["csts"].transpose(2, 1, 0).reshape(30, 512) for r in R], 0)[None]
    f_p = np.concatenate([r["fstp"].transpose(0, 3, 2, 1).reshape(NSEQ, 2, 2 * DFF) for r in R], 0)[None]
    f_s = np.stack([r["fsts"].transpose(2, 1, 0).reshape(2, 2 * DFF) for r in R], 0)[None]
    outs = (y_p, y_s, k_p, v_p, k_s, v_s, c_p, c_s, f_p, f_s)
    return tuple(np.ascontiguousarray(o, dtype=np.float32) for o in outs)


def kernel(**inputs):
    return run_cores(**inputs)
```
